# Optimizing a Trainium2 kernel written in Bass

```python
import jax, jax.numpy as jnp
from jax import lax
import numpy as np

D_MODEL = 1024
BATCH = 4
SEQ = 4096
DEPTH = 1

CTX_LEN = 256
GRID_W = 64
HG_HEADS = 8
HG_HEAD_DIM = 128
HG_WIDTH = HG_HEADS * HG_HEAD_DIM
CONV_GROUPS = 8
CONV_WIDTH = 1024
CONV_K = 3
MIX_WIDTH = HG_WIDTH + CONV_WIDTH
HG_COLS = 5 * HG_WIDTH
CONV_COLS = 4 * CONV_WIDTH
IN_COLS = HG_COLS + CONV_COLS
CHUNK = 64
EPS = 1e-6

kernel_name = 'hymba_hgrn2_shortconv_prefix_dit'


def rmsnorm(x, w):
    xf = x.astype(jnp.float32)
    y = xf * lax.rsqrt(jnp.mean(xf * xf, axis=-1, keepdims=True) + EPS)
    return (y * w.astype(jnp.float32)).astype(x.dtype)


def to_heads(z):
    b, t, _ = z.shape
    return z.reshape(b, t, HG_HEADS, HG_HEAD_DIM).transpose(0, 2, 1, 3)


def from_heads(z):
    b, h, t, d = z.shape
    return z.transpose(0, 2, 1, 3).reshape(b, t, h * d)


def lower_bound(logits, layer):
    sm = jax.nn.softmax(logits.astype(jnp.float32), axis=0)
    return jnp.cumsum(sm, axis=0)[layer]


def chunk_scan(q, k, v, logf, s0):
    b_, h_, t_, dk = q.shape
    dv = v.shape[-1]
    n = t_ // CHUNK
    split = lambda a: a.reshape(b_, h_, n, CHUNK, a.shape[-1]).transpose(2, 0, 1, 3, 4)
    mask = jnp.tril(jnp.ones((CHUNK, CHUNK), dtype=bool))[:, :, None]

    def step(S, inp):
        qc, kc, vc, gc = inp
        bcum = jnp.cumsum(gc, axis=2)
        diff = bcum[:, :, :, None, :] - bcum[:, :, None, :, :]
        decay = jnp.where(mask, jnp.exp(jnp.where(mask, diff, 0.0)), 0.0)
        attn = jnp.einsum('bhtk,bhsk,bhtsk->bhts', qc, kc, decay)
        o = jnp.einsum('bhts,bhsv->bhtv', attn, vc) + jnp.einsum('bhtk,bhkv->bhtv', qc * jnp.exp(bcum), S)
        b_last = bcum[:, :, -1, :]
        S_new = jnp.exp(b_last)[..., None] * S + jnp.einsum(
            'bhsk,bhsv->bhkv', kc * jnp.exp(b_last[:, :, None, :] - bcum), vc)
        return S_new, o

    s_fin, os_ = lax.scan(step, s0, (split(q), split(k), split(v), split(logf)))
    o = os_.transpose(1, 2, 0, 3, 4).reshape(b_, h_, t_, dv)
    return o, s_fin


def bidir_scan(q, k_f, k_b, v, g_f, g_b, s_f, s_b):
    flip = lambda a: jnp.flip(a, axis=2)
    o_f, sf = chunk_scan(q, k_f, v, g_f, s_f)
    o_b, sb = chunk_scan(flip(q), flip(k_b), flip(v), flip(g_b), s_b)
    return o_f + flip(o_b), sf, sb


def hgrn_prep(p_hg, lb_f, lb_b):
    q, i, zf, zb = [p_hg[..., j * HG_WIDTH:(j + 1) * HG_WIDTH].astype(jnp.float32) for j in range(4)]
    f_f = lb_f + (1.0 - lb_f) * jax.nn.sigmoid(zf)
    f_b = lb_b + (1.0 - lb_b) * jax.nn.sigmoid(zb)
    return (to_heads(q), to_heads(i), to_heads(1.0 - f_f), to_heads(jnp.log(f_f)),
            to_heads(1.0 - f_b), to_heads(jnp.log(f_b)))


def hgrn_out(o, gate, onorm_w, dtype):
    o = o * lax.rsqrt(jnp.mean(o * o, axis=-1, keepdims=True) + EPS) * onorm_w.astype(jnp.float32)
    return from_heads(o).astype(dtype) * jax.nn.silu(gate)


def conv3(u, w):
    pad = [(0, 0)] * (u.ndim - 2) + [(1, 1), (0, 0)]
    up = jnp.pad(u, pad)
    return w[0] * up[..., :-2, :] + w[1] * up[..., 1:-1, :] + w[2] * up[..., 2:, :]


def conv_branch(p_cv, w, grid):
    h, bg, cg, gate = [p_cv[..., j * CONV_WIDTH:(j + 1) * CONV_WIDTH] for j in range(4)]
    u = cg * h
    if grid:
        b_, t_, ch = u.shape
        rows = t_ // GRID_W
        v = conv3(u.reshape(b_, rows, GRID_W, ch), w).reshape(b_, t_, ch)
    else:
        v = conv3(u, w)
    return bg * v * jax.nn.silu(gate)


def setup_inputs(seed: int = 0) -> dict:
    key = jax.random.key(seed)
    ks = jax.random.split(key, 13)
    f32 = jnp.float32
    x = jax.random.normal(ks[0], (BATCH, SEQ, D_MODEL), f32)
    c = jax.random.normal(ks[1], (BATCH, D_MODEL), f32)
    ctx = jax.random.normal(ks[2], (BATCH, CTX_LEN, D_MODEL), f32)
    c_ctx = jax.random.normal(ks[3], (D_MODEL,), f32)
    norm_w = 1.0 + 0.02 * jax.random.normal(ks[4], (DEPTH, D_MODEL), f32)
    w_ada = 0.5 * D_MODEL ** -0.5 * jax.random.normal(ks[5], (DEPTH, D_MODEL, 3 * D_MODEL), f32)
    b_ada = 0.02 * jax.random.normal(ks[6], (DEPTH, 3 * D_MODEL), f32)
    w_in = D_MODEL ** -0.5 * jax.random.normal(ks[7], (DEPTH, D_MODEL, IN_COLS), f32)
    hg_lb_logits = 0.1 * jax.random.normal(ks[8], (2, DEPTH + 1, HG_WIDTH), f32)
    hg_onorm_w = 1.0 + 0.02 * jax.random.normal(ks[9], (DEPTH, HG_HEAD_DIM), f32)
    conv_w = CONV_K ** -0.5 * jax.random.normal(ks[10], (DEPTH, CONV_K, CONV_WIDTH), f32)
    w_out = MIX_WIDTH ** -0.5 * jax.random.normal(ks[11], (DEPTH, MIX_WIDTH, D_MODEL), f32)
    final_norm_w = 1.0 + 0.02 * jax.random.normal(ks[12], (D_MODEL,), f32)
    return {'x': x, 'c': c, 'ctx': ctx, 'c_ctx': c_ctx, 'norm_w': norm_w, 'w_ada': w_ada,
            'b_ada': b_ada, 'w_in': w_in, 'hg_lb_logits': hg_lb_logits, 'hg_onorm_w': hg_onorm_w,
            'conv_w': conv_w, 'w_out': w_out, 'final_norm_w': final_norm_w}


def reference(x, c, ctx, c_ctx, norm_w, w_ada, b_ada, w_in, hg_lb_logits, hg_onorm_w,
              conv_w, w_out, final_norm_w):
    xc = ctx
    bsz = x.shape[0]
    for l in range(DEPTH):
        last = l == DEPTH - 1
        shift, scale, gate = jnp.split(jax.nn.silu(c) @ w_ada[l] + b_ada[l], 3, axis=-1)
        shift_c, scale_c, gate_c = jnp.split(jax.nn.silu(c_ctx) @ w_ada[l] + b_ada[l], 3, axis=-1)
        h = rmsnorm(x, norm_w[l]) * (1.0 + scale[:, None]) + shift[:, None]
        hc = rmsnorm(xc, norm_w[l]) * (1.0 + scale_c) + shift_c
        p = h @ w_in[l]
        pc = hc @ (w_in[l, :, :HG_COLS] if last else w_in[l])
        lb_f = lower_bound(hg_lb_logits[0], l)
        lb_b = lower_bound(hg_lb_logits[1], l)
        qc, vc, kcf, gcf, kcb, gcb = hgrn_prep(pc[..., :HG_COLS], lb_f, lb_b)
        s0 = jnp.zeros((bsz, HG_HEADS, HG_HEAD_DIM, HG_HEAD_DIM), jnp.float32)
        oc, s_f, s_b = bidir_scan(qc, kcf, kcb, vc, gcf, gcb, s0, s0)
        q, v, kf, gf, kb, gb = hgrn_prep(p[..., :HG_COLS], lb_f, lb_b)
        o, _, _ = bidir_scan(q, kf, kb, v, gf, gb, s_f, s_b)
        y_hg = hgrn_out(o, p[..., 4 * HG_WIDTH:HG_COLS], hg_onorm_w[l], x.dtype)
        y_cv = conv_branch(p[..., HG_COLS:], conv_w[l], grid=True)
        y = jnp.concatenate([y_hg, y_cv], axis=-1) @ w_out[l]
        if not last:
            yc_hg = hgrn_out(oc, pc[..., 4 * HG_WIDTH:HG_COLS], hg_onorm_w[l], xc.dtype)
            yc_cv = conv_branch(pc[..., HG_COLS:], conv_w[l], grid=False)
            yc = jnp.concatenate([yc_hg, yc_cv], axis=-1) @ w_out[l]
            xc = xc + gate_c * yc
        x = x + gate[:, None] * y
    return rmsnorm(x, final_norm_w)
```

```python
import numpy as np
from contextlib import ExitStack
import concourse.bass as bass
import concourse.mybir as mybir
from concourse.bass_utils import run_bass_kernel_spmd

F32 = mybir.dt.float32
BF16 = mybir.dt.bfloat16
AF = mybir.ActivationFunctionType
ALU = mybir.AluOpType
EPS = 1e-6


class Tok:
    __slots__ = ("name", "w", "r")

    def __init__(self, name="", fence=None):
        self.name = name
        self.w = {}
        self.r = list(fence) if fence else []


class Sched:
    ENG = ("pe", "act", "dve", "pool", "sp")

    def __init__(self, nc, es, n_dma_sems=32, n_sw_sems=40):
        self.nc = nc
        self.h = {"pe": nc.tensor, "act": nc.scalar, "dve": nc.vector, "pool": nc.gpsimd, "sp": nc.sync}
        self.sem, self.count, self.clock, self.snap = {}, {}, {}, {}
        for e in self.ENG:
            self.sem[e] = es.enter_context(nc.semaphore(f"sem_{e}"))
            self.count[e] = 0
            self.clock[e] = {}
        self.dma_sems = []
        for i in range(n_dma_sems):
            tl = f"dma{i}"
            self.sem[tl] = es.enter_context(nc.semaphore(f"sem_{tl}"))
            self.count[tl] = 0
            self.dma_sems.append(tl)
        self.sw_sems = []
        for i in range(n_sw_sems):
            tl = f"dmasw{i}"
            self.sem[tl] = es.enter_context(nc.semaphore(f"sem_{tl}"))
            self.count[tl] = 0
            self.sw_sems.append(tl)
        self.sw_next = 0
        self.dma_rr = 0
        self.n_wait = 0
        self.n_inst = 0
        self.cur_fence = []

    def tok(self, name=""):
        return Tok(name, self.cur_fence)

    def toks(self, name, n):
        return [self.tok(f"{name}{i}") for i in range(n)]

    def fence(self):
        self.cur_fence = [(tl, c) for tl, c in self.count.items() if c > 0]

    def _wait(self, eng, ev):
        tl, cnt = ev
        ck = self.clock[eng]
        if ck.get(tl, 0) >= cnt:
            return
        mult = 16 if tl.startswith("dma") else 1
        self.h[eng].wait_ge(self.sem[tl], cnt * mult)
        self.n_wait += 1
        sn = self.snap.get(ev)
        if sn:
            for k, v in sn.items():
                if ck.get(k, 0) < v:
                    ck[k] = v
        ck[tl] = cnt

    def _deps(self, reads, writes):
        best = {}

        def add(tl, cnt):
            if best.get(tl, 0) < cnt:
                best[tl] = cnt
        for t in reads:
            for tl, cnt in t.w.items():
                add(tl, cnt)
        for t in writes:
            for tl, cnt in t.w.items():
                add(tl, cnt)
            for tl, cnt in t.r:
                add(tl, cnt)
        return best

    def _commit(self, me, eng, reads, writes):
        self.snap[me] = dict(self.clock[eng])
        for t in reads:
            t.r.append(me)
        for t in writes:
            t.w[me[0]] = me[1]
            t.r = []

    def op(self, eng, fn, reads=(), writes=()):
        for tl, cnt in self._deps(reads, writes).items():
            if tl == eng and eng == "pe":
                continue
            self._wait(eng, (tl, cnt))
        inst = fn(self.h[eng])
        self.count[eng] += 1
        inst.then_inc(self.sem[eng], 1)
        self.n_inst += 1
        me = (eng, self.count[eng])
        self._commit(me, eng, reads, writes)
        return me

    def dma(self, eng, out, in_, reads=(), writes=(), **kw):
        for tl, cnt in self._deps(reads, writes).items():
            self._wait(eng, (tl, cnt))
        if eng == "pool":
            tl = self.sw_sems[self.sw_next]
            self.sw_next += 1
        else:
            tl = self.dma_sems[self.dma_rr % len(self.dma_sems)]
            self.dma_rr += 1
            if self.count[tl] > 0:
                self._wait(eng, (tl, self.count[tl]))
        inst = self.h[eng].dma_start(out=out, in_=in_, **kw)
        self.count[tl] += 1
        inst.then_inc(self.sem[tl], 16)
        self.n_inst += 1
        me = (tl, self.count[tl])
        self._commit(me, eng, reads, writes)
        return me

    def wait_all(self, eng, tokens):
        best = {}
        for t in tokens:
            for tl, cnt in list(t.w.items()) + list(t.r):
                if best.get(tl, 0) < cnt:
                    best[tl] = cnt
        for tl, cnt in best.items():
            self._wait(eng, (tl, cnt))


class Pool_:
    def __init__(self, S, nc, es, name, n, shape, dtype):
        self.tiles = [es.enter_context(nc.sbuf_tensor(f"{name}{i}", shape, dtype)) for i in range(n)]
        self.toks = [S.tok(f"{name}{i}") for i in range(n)]
        self.i = 0

    def get(self):
        k = self.i % len(self.tiles)
        self.i += 1
        return self.tiles[k], self.toks[k]


class Cfg:
    def __init__(self, D=1024, NH=8, NCC=8, T_OWN=2048, T_PRE=2048, T_CTX=256, GRID_W=64):
        self.D, self.NH, self.NCC = D, NH, NCC
        self.T_OWN, self.T_PRE, self.T_CTX, self.GRID_W = T_OWN, T_PRE, T_CTX, GRID_W
        self.KD = D // 128
        self.NM = NH + NCC
        self.HALF = min(512, D)
        self.NHALF = D // self.HALF


FULL = Cfg()


def build(cfg):
    D, NH, NCC, KD, NM = cfg.D, cfg.NH, cfg.NCC, cfg.KD, cfg.NM
    T_OWN, T_PRE, T_CTX = cfg.T_OWN, cfg.T_PRE, cfg.T_CTX
    HALF, NHALF = cfg.HALF, cfg.NHALF
    NBO, NBP = T_OWN // 512, T_PRE // 512
    NCO, NCP, NCX = T_OWN // 128, T_PRE // 128, T_CTX // 128
    GW = cfg.GRID_W

    nc = bass.Bass("TRN2", target_bir_lowering=False)
    dt = lambda n, s, k="ExternalInput": nc.dram_tensor(n, s, F32, kind=k).ap()
    x_pre, x_own, x_ctx = dt("x_pre", [T_PRE, D]), dt("x_own", [T_OWN, D]), dt("x_ctx", [T_CTX, D])
    cvec_d = dt("cvec", [128, KD * 2])
    wada_d = dt("wada", [3, 128, KD * D])
    badafm_d = dt("bada_fm", [128, 2 * KD])
    badag_d = dt("bada_gate_bc", [128, D])
    normw_d = dt("normw_fm", [128, KD])
    lbl_d = dt("lbl", [128, 4 * NH])
    onw_d = dt("onw", [128, 1])
    convw_d = dt("convw", [128, 3 * NCC])
    wpre_d = dt("w_pre", [NH, 128, 3 * KD * 128])
    whd_d = dt("w_hd", [NH, 128, 5 * KD * 128])
    wcv_d = dt("w_cv", [NCC, 128, 4 * KD * 128])
    wout_d = dt("w_out", [128, NM * D])
    fw_d = dt("fw_bc", [128, D])
    out_d = dt("out", [T_OWN, D], "ExternalOutput")

    with ExitStack() as es:
        S = Sched(nc, es)
        sb = lambda st, n, s, d=F32: st.enter_context(nc.sbuf_tensor("s_" + n, s, d))

        ident_f = sb(es, "ident_f", [128, 128]); ident_b = sb(es, "ident_b", [128, 128], BF16)
        ones_f = sb(es, "ones_f", [128, 128])
        mask2 = sb(es, "mask2", [128, 2, 128])
        maskA = mask2[:, 0, :]; maskB = mask2[:, 1, :]
        scanmask = sb(es, "scanmask", [128, 512])
        hT_own = sb(es, "hT_own", [128, KD, T_OWN], BF16)
        Sst = sb(es, "Sst", [128, 2 * NH, 128])
        gate_bc = sb(es, "gate_bc", [128, D]); fw_bc = sb(es, "fw_bc", [128, D])
        modsb = sb(es, "modsb", [128, 2 * KD, 2]); gvec = sb(es, "gvec", [128, KD, 2])
        lbl = sb(es, "lbl", [128, 2, 2, NH]); lbd = sb(es, "lbd", [128, 2, NH])
        oml = sb(es, "oml", [128, 2, NH]); noml = sb(es, "noml", [128, 2, NH])
        onw = sb(es, "onw", [128, 1]); convw = sb(es, "convw", [128, 3, NCC])
        normw = sb(es, "normw", [128, KD]); badafm = sb(es, "badafm", [128, 2 * KD])
        t_const = S.tok("const")
        t_hTo = S.toks("hTo", NBO)
        t_hTc = S.tok("hTc")
        t_S = S.toks("S", 2 * NH)
        t_gate, t_fw, t_mod, t_lb, t_misc = S.tok("gate"), S.tok("fw"), S.tok("mod"), S.tok("lb"), S.tok("misc")

        pb = [es.enter_context(nc.psum_tensor(f"pb{i}", [128, 512], F32)) for i in range(7)]
        pbt = es.enter_context(nc.psum_tensor("pbt", [128, 1024], BF16))
        t_pb = [S.tok(f"pb{i}") for i in range(7)]
        t_pb4q = [t_pb[4]] * 4
        t_pb5q = [t_pb[5]] * 4

        P = "pool"
        S.op(P, lambda e: e.memset(ident_f[:], 0.0), writes=[t_const])
        S.op(P, lambda e: e.affine_select(out=ident_f[:], in_=ident_f[:], pattern=[[-1, 128]], compare_op=ALU.not_equal,
                                          fill=1.0, base=0, channel_multiplier=1), reads=[t_const], writes=[t_const])
        S.op(P, lambda e: e.tensor_copy(out=ident_b[:], in_=ident_f[:]), reads=[t_const], writes=[t_const])
        S.op(P, lambda e: e.memset(ones_f[:], 1.0), writes=[t_const])
        S.op(P, lambda e: e.memset(maskA, 1.0), writes=[t_const])
        S.op(P, lambda e: e.affine_select(out=maskA, in_=maskA, pattern=[[1, 128]], compare_op=ALU.is_ge,
                                          fill=0.0, base=0, channel_multiplier=-1), reads=[t_const], writes=[t_const])
        S.op(P, lambda e: e.memset(maskB, 1.0), writes=[t_const])
        S.op(P, lambda e: e.affine_select(out=maskB, in_=maskB, pattern=[[-1, 128]], compare_op=ALU.is_ge,
                                          fill=0.0, base=0, channel_multiplier=1), reads=[t_const], writes=[t_const])
        S.op(P, lambda e: e.memset(scanmask[:], 1.0), writes=[t_const])
        S.op(P, lambda e: e.memset(scanmask[:].rearrange("p (c t) -> p c t", t=128)[:, :, 0:1], 0.0), writes=[t_const])
        S.op(P, lambda e: e.memset(Sst[:], 0.0), writes=t_S)

        S.dma("sp", lbl[:].rearrange("p a b c -> p (a b c)"), lbl_d[:, :], writes=[t_lb])
        S.dma("sp", onw[:], onw_d[:, :], writes=[t_misc])
        S.dma("sp", convw[:].rearrange("p a b -> p (a b)"), convw_d[:, :], writes=[t_misc])
        S.dma("sp", normw[:], normw_d[:, :], writes=[t_misc])
        S.dma("sp", badafm[:], badafm_d[:, :], writes=[t_misc])
        S.dma("sp", fw_bc[:], fw_d[:, :], writes=[t_fw])
        S.op("dve", lambda e: e.tensor_tensor(out=lbd[:], in0=lbl[:, :, 0, :], in1=lbl[:, :, 1, :], op=ALU.subtract),
             reads=[t_lb], writes=[t_lb])
        S.op("act", lambda e: e.activation(out=oml[:], in_=lbd[:], func=AF.Sigmoid, scale=-1.0), reads=[t_lb], writes=[t_lb])
        S.op("dve", lambda e: e.tensor_scalar(out=noml[:], in0=oml[:], scalar1=-1.0, scalar2=None, op0=ALU.mult),
             reads=[t_lb], writes=[t_lb])

        scs = Pool_(S, nc, es, "scs_", 4, [128, 12], F32)
        st1 = ExitStack()
        hT_pre = sb(st1, "hT_pre", [128, KD, T_PRE], BF16)
        hT_ctx = sb(st1, "hT_ctx", [128, KD, T_CTX], BF16)
        t_hTp = S.toks("hTp", NBP)

        with ExitStack() as ph:
            wada_sb = [sb(ph, f"wada{i}", [128, KD, D]) for i in range(2)]
            t_wada = S.toks("wada", 2)
            cv = sb(ph, "cv", [128, KD, 2]); csg = sb(ph, "csg", [128, KD, 2]); sc2 = sb(ph, "sc2", [128, KD, 2])
            screp = sb(ph, "screp", [128, KD, 128])
            t_c = S.tok("c")
            NXT = 8
            xt = [sb(ph, f"xt{i}", [128, D]) for i in range(NXT)]
            t_xt = S.toks("xt", NXT)
            junk = sb(ph, "junk", [128, D]); t_junk = S.tok("junk")
            NT_ALL = NCP + NCX + NCO
            ssq = sb(ph, "ssq", [128, NT_ALL]); lnv = sb(ph, "lnv", [128, NT_ALL]); rstd = sb(ph, "rstd", [128, NT_ALL])
            t_ss = S.toks("ss", NT_ALL)

            S.dma("sp", cv[:].rearrange("p a b -> p (a b)"), cvec_d[:, :], writes=[t_c])
            S.op("act", lambda e: e.activation(out=csg[:], in_=cv[:], func=AF.Sigmoid), reads=[t_c], writes=[t_c])
            S.op("dve", lambda e: e.tensor_tensor(out=sc2[:], in0=cv[:], in1=csg[:], op=ALU.mult), reads=[t_c], writes=[t_c])
            S.op("dve", lambda e: e.tensor_copy(out=screp[:], in_=sc2[:, :, 0:1].to_broadcast([128, KD, 128])),
                 reads=[t_c], writes=[t_c])
            for kind in range(3):
                wt, tw = wada_sb[kind % 2], t_wada[kind % 2]
                S.dma("act" if kind % 2 else "sp", wt[:].rearrange("p a b -> p (a b)"), wada_d[kind, :, :], writes=[tw])
                if kind < 2:
                    for jo in range(KD):
                        col = (kind * KD + jo) * 2
                        for ji in range(KD):
                            S.op("pe", lambda e, ji=ji, jo=jo, col=col, wt=wt: e.matmul(
                                pb[6][:, col:col + 2], lhsT=wt[:, ji, jo * 128:(jo + 1) * 128], rhs=sc2[:, ji, :],
                                start=(ji == 0), stop=(ji == KD - 1)), reads=[tw, t_c], writes=[t_pb[6]])
                else:
                    for hf in range(NHALF):
                        for ji in range(KD):
                            S.op("pe", lambda e, ji=ji, hf=hf, wt=wt: e.matmul(
                                pb[4 + hf][:, 0:HALF], lhsT=screp[:, ji, :], rhs=wt[:, ji, hf * HALF:(hf + 1) * HALF],
                                start=(ji == 0), stop=(ji == KD - 1)), reads=[tw, t_c], writes=[t_pb[4 + hf]])
            S.op("dve", lambda e: e.tensor_tensor(
                out=modsb[:], in0=pb[6][:, 0:4 * KD].rearrange("p (a b) -> p a b", b=2),
                in1=badafm[:].rearrange("p (a o) -> p a o", o=1).to_broadcast([128, 2 * KD, 2]), op=ALU.add),
                reads=[t_pb[6], t_misc], writes=[t_mod])
            S.op("dve", lambda e: e.scalar_tensor_tensor(
                out=gvec[:], in0=modsb[:, KD:2 * KD, :], scalar=1.0,
                in1=normw[:].rearrange("p (a o) -> p a o", o=1).to_broadcast([128, KD, 2]), op0=ALU.add, op1=ALU.mult),
                reads=[t_mod, t_misc], writes=[t_mod])
            gb = sb(ph, "gb", [128, D]); t_gb = S.tok("gb")
            S.dma("sp", gb[:], badag_d[:, :], writes=[t_gb])
            for hf in range(NHALF):
                S.op("dve", lambda e, hf=hf: e.tensor_tensor(out=gate_bc[:, hf * HALF:(hf + 1) * HALF], in0=pb[4 + hf][:, 0:HALF],
                                                             in1=gb[:, hf * HALF:(hf + 1) * HALF], op=ALU.add),
                     reads=[t_pb[4 + hf], t_gb], writes=[t_gate])

            segs = [(x_pre, hT_pre, t_hTp, T_PRE, 0, 0), (x_ctx, hT_ctx, [t_hTc], T_CTX, 1, NCP),
                    (x_own, hT_own, t_hTo, T_OWN, 0, NCP + NCX)]
            xi = 0
            evq = 0
            for (xd, hT, thT, TT, which, tbase) in segs:
                for g0 in range(0, TT // 128, 4):
                    G = min(4, TT // 128 - g0)
                    bufs = []
                    for i in range(G):
                        k = xi % NXT
                        xi += 1
                        ti = g0 + i
                        gi = tbase + ti
                        S.dma("sp", xt[k][:], xd[ti * 128:(ti + 1) * 128, :], writes=[t_xt[k]])
                        S.op("act", lambda e, k=k, gi=gi: e.activation(out=junk[:], in_=xt[k][:], func=AF.Square,
                                                                       accum_out=ssq[:, gi:gi + 1]),
                             reads=[t_xt[k]], writes=[t_junk, t_ss[gi]])
                        S.op("act", lambda e, gi=gi: e.activation(out=lnv[:, gi:gi + 1], in_=ssq[:, gi:gi + 1], func=AF.Ln,
                                                                  scale=1.0 / D, bias=EPS), reads=[t_ss[gi]], writes=[t_ss[gi]])
                        S.op("act", lambda e, gi=gi: e.activation(out=rstd[:, gi:gi + 1], in_=lnv[:, gi:gi + 1], func=AF.Exp,
                                                                  scale=-0.5), reads=[t_ss[gi]], writes=[t_ss[gi]])
                        S.op("dve", lambda e, k=k, gi=gi: e.tensor_scalar(out=xt[k][:], in0=xt[k][:], scalar1=rstd[:, gi:gi + 1],
                                                                          scalar2=None, op0=ALU.mult),
                             reads=[t_xt[k], t_ss[gi]], writes=[t_xt[k]])
                        bufs.append(k)
                    blk = g0 // 4
                    for j in range(KD):
                        bank = j % 2
                        for i, k in enumerate(bufs):
                            S.op("pe", lambda e, i=i, k=k, j=j, bank=bank: e.transpose(
                                out=pb[bank][:, i * 128:(i + 1) * 128], in_=xt[k][:, j * 128:(j + 1) * 128], identity=ident_f[:]),
                                reads=[t_xt[k], t_const], writes=[t_pb[bank]])
                        dst = hT[:, j, g0 * 128:(g0 + G) * 128]
                        src = pb[bank][:, 0:G * 128]
                        if evq % 2 == 0:
                            S.op("act", lambda e, dst=dst, src=src, j=j, which=which: e.activation(
                                out=dst, in_=src, func=AF.Identity, scale=gvec[:, j, which:which + 1],
                                bias=modsb[:, j, which:which + 1]), reads=[t_pb[bank], t_mod], writes=[thT[blk]])
                        else:
                            S.op("dve", lambda e, dst=dst, src=src, j=j, which=which: e.tensor_scalar(
                                out=dst, in0=src, scalar1=gvec[:, j, which:which + 1], scalar2=modsb[:, j, which:which + 1],
                                op0=ALU.mult, op1=ALU.add), reads=[t_pb[bank], t_mod], writes=[thT[blk]])
                        evq += 1
        S.fence()

        def inproj_fm(bank, w_ap_fn, hT, tok0, n, t_w, t_h):
            for j in range(KD):
                S.op("pe", lambda e, j=j: e.matmul(pb[bank][:, 0:n], lhsT=w_ap_fn(j), rhs=hT[:, j, tok0:tok0 + n],
                                                   start=(j == 0), stop=(j == KD - 1)),
                     reads=[t_w] + t_h, writes=[t_pb[bank]])

        def vproj(w_ap_fn, hT, tok0, ntile, vdst, vt0, t_w, t_h, t_v, on_dve=False, alt_bank=False):
            vb, t_vb = (pbt[:, :].bitcast(F32), t_pbt) if alt_bank else (pb[4][:, :], t_pb[4])
            for i in range(ntile):
                for j in range(KD):
                    S.op("pe", lambda e, i=i, j=j: e.matmul(
                        vb[:, i * 128:(i + 1) * 128], lhsT=hT[:, j, tok0 + i * 128:tok0 + (i + 1) * 128], rhs=w_ap_fn(j),
                        start=(j == 0), stop=(j == KD - 1)), reads=[t_w] + t_h, writes=[t_vb])
            if on_dve:
                S.op("dve", lambda e: e.tensor_copy(out=vdst[:, vt0:vt0 + ntile, :],
                                                    in_=vb[:, 0:ntile * 128].rearrange("p (a b) -> p a b", b=128)),
                     reads=[t_vb], writes=[t_v])
            else:
                S.op("act", lambda e: e.activation(out=vdst[:, vt0:vt0 + ntile, :],
                                                   in_=vb[:, 0:ntile * 128].rearrange("p (a b) -> p a b", b=128), func=AF.Copy),
                     reads=[t_vb], writes=[t_v])

        t_pbt = S.tok("pbt")

        with ExitStack() as ph:
            wp = [sb(ph, f"wp{i}", [128, 3, KD, 128], BF16) for i in range(2)]
            t_wp = S.toks("wp", 2)
            scr = Pool_(S, nc, ph, "scr1_", 12, [128, 516], F32)
            ones512 = sb(ph, "ones512", [128, 512]); t_ones = S.tok("ones512")
            S.op("pool", lambda e: e.memset(ones512[:], 1.0), writes=[t_ones])
            v_pre = [sb(ph, f"v_pre{i}", [128, NCP, 128], BF16) for i in range(2)]
            t_vp = S.toks("vp", 2)
            v_ctx = [sb(ph, f"v_ctx{i}", [128, NCX, 128], BF16) for i in range(2)]
            t_vc = S.toks("vc", 2)
            khat = Pool_(S, nc, ph, "khat_", 4, [128, 512], BF16)
            ktokb = [sb(ph, f"ktokb{i}", [128, 4, 128], BF16) for i in range(4)]
            t_ktokb = S.toks("ktokb", 4)
            tsc = Pool_(S, nc, ph, "tsc_", 8, [128, 2], F32)

            def load_wp(hd):
                S.dma("pool", wp[hd % 2][:].rearrange("p a b c -> p (a b c)"), wpre_d[hd, :, :], writes=[t_wp[hd % 2]],
                      max_dma_last_dim=4096)
            all_groups = []
            for hd in range(NH):
                p2 = hd % 2
                jobs = [(hd, 0, hT_ctx, [t_hTc], 0, T_CTX, "suf", 0, hd * 2 + 0, v_ctx[p2], t_vc[p2], 0, True, True),
                        (hd, 1, hT_ctx, [t_hTc], 0, T_CTX, "exc", 1, hd * 2 + 1, v_ctx[p2], t_vc[p2], 0, True, False)]
                for blk in range(NBP):
                    jobs.append((hd, 0, hT_pre, [t_hTp[blk]], blk * 512, 512, "suf", 0, hd * 2 + 0, v_pre[p2], t_vp[p2], blk * 4, False, True))
                for g0 in range(0, len(jobs), 2):
                    all_groups.append(jobs[g0:g0 + 2])
            loaded = set()

            def ensure_w(hd):
                for h in (hd, hd + 1):
                    if h < NH and h not in loaded:
                        load_wp(h)
                        loaded.add(h)

            def P1(gi, grp):
                zb0 = 0 if gi % 2 == 0 else 2
                st = []
                for bi, (hd, wg, hT, th, tok0, n, form, lbdir, si, vbuf, t_v, vt0, first, do_v) in enumerate(grp):
                    ensure_w(hd)
                    w, tw = wp[hd % 2], t_wp[hd % 2]
                    zb = zb0 + bi
                    inproj_fm(zb, lambda j, wg=wg, w=w: w[:, wg, j, :], hT, tok0, n, tw, th)
                    sg, t_sg = scr.get()
                    S.op("act", lambda e, sg=sg, zb=zb, n=n: e.activation(out=sg[:, 0:n], in_=pb[zb][:, 0:n], func=AF.Sigmoid, scale=-1.0),
                         reads=[t_pb[zb]], writes=[t_sg])
                    if do_v:
                        vproj(lambda j, w=w: w[:, 2, j, :], hT, tok0, n // 128, vbuf, vt0, tw, th, t_v, on_dve=True, alt_bank=(bi == 1))
                    st.append([sg, t_sg])
                return st

            def P2e(gi, grp, st):
                for bi, (hd, wg, hT, th, tok0, n, form, lbdir, si, vbuf, t_v, vt0, first, do_v) in enumerate(grp):
                    sg, t_sg = st[bi][0], st[bi][1]
                    lf, t_lf = scr.get(); bb, t_bb = scr.get()
                    kh, t_kh = khat.get()
                    e1t, t_e1 = tsc.get()
                    st[bi] += [lf, t_lf, bb, t_bb, kh, t_kh, e1t, t_e1]
                    if form == "exc":
                        S.op("pool", lambda e, lf=lf: e.memset(lf[:, 0:1], 0.0), writes=[t_lf])
                    S.op("act", lambda e, lf=lf, sg=sg, n=n, lbdir=lbdir, hd=hd: e.activation(
                        out=lf[:, 1:n + 1], in_=sg[:, 0:n], func=AF.Ln, scale=noml[:, lbdir, hd:hd + 1], bias=1.0),
                        reads=[t_sg, t_lb], writes=[t_lf])
                for bi, (hd, wg, hT, th, tok0, n, form, lbdir, si, vbuf, t_v, vt0, first, do_v) in enumerate(grp):
                    sg, t_sg, lf, t_lf, bb, t_bb, kh, t_kh, e1t, t_e1 = st[bi]
                    if form == "suf":
                        S.op("dve", lambda e, lf=lf, bb=bb, n=n: e.tensor_tensor_scan(
                            out=bb[:, 0:n], data0=ones512[:, 0:n], data1=lf[:, 1:n + 1], initial=0.0, op0=ALU.mult, op1=ALU.add),
                            reads=[t_lf, t_ones], writes=[t_bb])
                        S.op("dve", lambda e, bb=bb, e1t=e1t, n=n: e.tensor_copy(out=e1t[:, 1:2], in_=bb[:, n - 1:n]),
                             reads=[t_bb], writes=[t_e1])
                    else:
                        S.op("dve", lambda e, lf=lf, bb=bb, n=n: e.tensor_tensor_scan(
                            out=bb[:, 0:n], data0=lf[:, 0:n], data1=ones512[:, 0:n], initial=0.0, op0=ALU.add, op1=ALU.mult),
                            reads=[t_lf, t_ones], writes=[t_bb])
                for bi, (hd, wg, hT, th, tok0, n, form, lbdir, si, vbuf, t_v, vt0, first, do_v) in enumerate(grp):
                    sg, t_sg, lf, t_lf, bb, t_bb, kh, t_kh, e1t, t_e1 = st[bi]
                    if form == "suf":
                        if not first:
                            S.op("act", lambda e, e1t=e1t: e.activation(out=e1t[:, 0:1], in_=e1t[:, 1:2], func=AF.Exp),
                                 reads=[t_e1], writes=[t_e1])
                        S.op("act", lambda e, bb=bb, e1t=e1t, n=n: e.activation(out=bb[:, 0:n], in_=bb[:, 0:n], func=AF.Exp,
                                                                               scale=-1.0, bias=e1t[:, 1:2]),
                             reads=[t_bb, t_e1], writes=[t_bb])
                    else:
                        S.op("act", lambda e, bb=bb, n=n: e.activation(out=bb[:, 0:n], in_=bb[:, 0:n], func=AF.Exp),
                             reads=[t_bb], writes=[t_bb])
                for bi, (hd, wg, hT, th, tok0, n, form, lbdir, si, vbuf, t_v, vt0, first, do_v) in enumerate(grp):
                    sg, t_sg, lf, t_lf, bb, t_bb, kh, t_kh, e1t, t_e1 = st[bi]
                    S.op("dve", lambda e, kh=kh, sg=sg, bb=bb, n=n, lbdir=lbdir, hd=hd: e.scalar_tensor_tensor(
                        out=kh[:, 0:n], in0=sg[:, 0:n], scalar=oml[:, lbdir, hd:hd + 1], in1=bb[:, 0:n], op0=ALU.mult, op1=ALU.mult),
                        reads=[t_sg, t_bb, t_lb], writes=[t_kh])

            def P2pe(gi, grp, st):
                zb0 = 0 if gi % 2 == 0 else 2
                kslot = (gi % 2) * 2
                pbanks = [(pb[5][:, 0:128], t_pb[5]), (pb[6][:, 0:128], t_pb[6])]
                for bi, (hd, wg, hT, th, tok0, n, form, lbdir, si, vbuf, t_v, vt0, first, do_v) in enumerate(grp):
                    sg, t_sg, lf, t_lf, bb, t_bb, kh, t_kh, e1t, t_e1 = st[bi]
                    zb = zb0 + bi
                    tb = pb[zb][:, :].bitcast(BF16)
                    for i in range(n // 128):
                        S.op("pe", lambda e, i=i, kh=kh, tb=tb: e.transpose(out=tb[:, i * 128:(i + 1) * 128], in_=kh[:, i * 128:(i + 1) * 128],
                                                                          identity=ident_b[:]), reads=[t_kh, t_const], writes=[t_pb[zb]])
                for bi, (hd, wg, hT, th, tok0, n, form, lbdir, si, vbuf, t_v, vt0, first, do_v) in enumerate(grp):
                    nch = n // 128
                    zb = zb0 + bi
                    tb = pb[zb][:, :].bitcast(BF16)
                    kb, t_kb = ktokb[kslot + bi], t_ktokb[kslot + bi]
                    S.op("dve", lambda e, kb=kb, nch=nch, n=n, tb=tb: e.tensor_copy(
                        out=kb[:, 0:nch, :], in_=tb[:, 0:n].rearrange("p (a b) -> p a b", b=128)),
                        reads=[t_pb[zb]], writes=[t_kb])
                for bi, (hd, wg, hT, th, tok0, n, form, lbdir, si, vbuf, t_v, vt0, first, do_v) in enumerate(grp):
                    nch = n // 128
                    kb, t_kb = ktokb[kslot + bi], t_ktokb[kslot + bi]
                    pap, t_p = pbanks[bi]
                    for i in range(nch):
                        S.op("pe", lambda e, i=i, kb=kb, vbuf=vbuf, vt0=vt0, nch=nch, pap=pap: e.matmul(
                            pap, lhsT=kb[:, i, :], rhs=vbuf[:, vt0 + i, :], start=(i == 0), stop=(i == nch - 1)),
                            reads=[t_kb, t_v], writes=[t_p])
                for bi, (hd, wg, hT, th, tok0, n, form, lbdir, si, vbuf, t_v, vt0, first, do_v) in enumerate(grp):
                    sg, t_sg, lf, t_lf, bb, t_bb, kh, t_kh, e1t, t_e1 = st[bi]
                    pap, t_p = pbanks[bi]
                    if first:
                        S.op("dve", lambda e, si=si, pap=pap: e.tensor_copy(out=Sst[:, si, :], in_=pap),
                             reads=[t_p], writes=[t_S[si]])
                    else:
                        S.op("dve", lambda e, si=si, e1t=e1t, pap=pap: e.scalar_tensor_tensor(
                            out=Sst[:, si, :], in0=Sst[:, si, :], scalar=e1t[:, 0:1], in1=pap, op0=ALU.mult, op1=ALU.add),
                            reads=[t_p, t_e1, t_S[si]], writes=[t_S[si]])

            prev = None
            for gi, grp in enumerate(all_groups):
                st = P1(gi, grp)
                if prev is not None:
                    P2pe(*prev)
                P2e(gi, grp, st)
                prev = (gi, grp, st)
            P2pe(*prev)
        st1.close()
        S.fence()

        ymixT = sb(es, "ymixT", [128, NM, T_OWN], BF16)
        t_ym = [S.toks(f"ym{m}_", NBO) for m in range(NM)]

        with ExitStack() as ph:
            wh = [sb(ph, f"wh{i}", [128, 5, KD, 128], BF16) for i in range(2)]
            t_wh = S.toks("wh", 2)
            scr = Pool_(S, nc, ph, "scr2_", 12, [128, 516], F32)
            for tl, tt in zip(scr.tiles, scr.toks):
                S.op("pool", lambda e, tl=tl: e.memset(tl[:], 0.0), writes=[tt])
            v_hd = [sb(ph, f"v_hd{i}", [128, NCO, 128], BF16) for i in range(1)]
            t_vh = S.toks("vh", 1)
            Q = sb(ph, "qk", [128, 6, T_OWN], BF16)
            TQ = [S.toks(f"qk{a}_", NBO) for a in range(6)]
            sqb = [sb(ph, f"sqb{i}", [128, 512], BF16) for i in range(2)]; t_sqb = S.toks("sqb", 2)
            ones_b = sb(ph, "ones_b", [128, 128], BF16); t_onesb = S.tok("ones_b")
            S.op("pool", lambda e: e.memset(ones_b[:], 1.0), writes=[t_onesb])
            silu_g = sb(ph, "silu_g", [128, T_OWN], BF16); t_sil = S.toks("sil", NBO)
            escA = sb(ph, "escA", [128, 3, NCO]); escB = sb(ph, "escB", [128, 3, NCO])
            t_escA, t_escB = S.tok("escA"), S.tok("escB")
            SpB = sb(ph, "SpB", [128, NCO, 128], BF16); t_SpB = S.toks("SpB", NCO)
            SpA = [sb(ph, f"SpA{i}", [128, 128], BF16) for i in range(2)]; t_SpA = S.toks("SpA", 2)
            atm = [sb(ph, f"atm{i}", [128, 2, 128], BF16) for i in range(2)]; t_atm = S.toks("atm", 2)
            ktokb = [sb(ph, f"ktokc{i}", [128, 4, 128], BF16) for i in range(2)]
            t_ktokb = S.toks("ktokc", 2)
            Spp = sb(ph, "Spp", [128, 2, 3, 128]); t_Spp = [S.toks("SppA", 3), S.toks("SppB", 3)]
            grpctr = [0]

            class Ring:
                def __init__(self, items):
                    self.items, self.i = items, 0

                def get(self):
                    it = self.items[self.i % len(self.items)]
                    self.i += 1
                    return it
            xt_tiles = []
            for p in range(min(8, NCC)):
                pl = ymixT[:, NH + p, :].bitcast(F32)
                for hh in range(T_OWN // 1024):
                    xt_tiles.append((pl[:, hh * 512:(hh + 1) * 512], S.tok(f"xscr{p}_{hh}")))
            assert len(xt_tiles) >= 4 + 3 * NBO, "need the 8 conv planes of ymixT as scratch"
            xr_sgA, xr_sgB = Ring(xt_tiles[0:2]), Ring(xt_tiles[2:4])
            xr_qf, xr_bbA, xr_bbB = [Ring(xt_tiles[4 + i * NBO:4 + (i + 1) * NBO]) for i in range(3)]
            xscr_toks = [t for _, t in xt_tiles]

            def load_wh(hd):
                S.dma("pool", wh[hd % 2][:].rearrange("p a b c -> p (a b c)"), whd_d[hd, :, :], writes=[t_wh[hd % 2]],
                      max_dma_last_dim=4096)

            def bulk_P(qi, cg, bank, tb, t_tb, vh, t_v):
                gn = grpctr[0]
                grpctr[0] += 1
                kb, t_kb = ktokb[gn % 2], t_ktokb[gn % 2]
                for i in range(4):
                    c = cg * 4 + i
                    S.op("pe", lambda e, i=i, c=c: e.transpose(out=tb[:, i * 128:(i + 1) * 128], in_=Q[:, qi, c * 128:(c + 1) * 128],
                                                              identity=ident_b[:]), reads=[TQ[qi][cg], t_const], writes=[t_tb])
                S.op("act", lambda e: e.activation(out=kb[:], in_=tb[:, 0:512].rearrange("p (a b) -> p a b", b=128), func=AF.Copy),
                     reads=[t_tb], writes=[t_kb])
                for i in range(4):
                    c = cg * 4 + i
                    S.op("pe", lambda e, i=i, c=c: e.matmul(pb[bank][:, i * 128:(i + 1) * 128], lhsT=kb[:, i, :], rhs=vh[:, c, :],
                                                            start=True, stop=True), reads=[t_kb, t_v], writes=[t_pb[bank]])

            load_wh(0)
            for hd in range(NH):
                if hd + 1 < NH:
                    load_wh(hd + 1)
                w, tw = wh[hd % 2], t_wh[hd % 2]
                p2 = hd % 2
                vh, t_v = v_hd[0], t_vh[0]
                siA, siB = hd * 2, hd * 2 + 1
                NG = NCO // 4
                Bst = {"cur": Sst[:, siB, :], "t": t_S[siB], "k": 0}

                def B_bulk(cg, k=0, vh=vh, t_v=t_v):
                    tb, t_tb = (pbt, t_pbt) if k % 2 == 0 else (pb[5][:, :].bitcast(BF16), t_pb[5])
                    bulk_P(5, cg, cg % 4, tb, t_tb, vh, t_v)

                def B_stage(cg, Bst=Bst, vh=vh, t_v=t_v, siB=siB, with_bulk=True):
                    pbk = cg % 4
                    if with_bulk:
                        bulk_P(5, cg, pbk, pbt, t_pbt, vh, t_v)
                    for c in reversed(range(cg * 4, cg * 4 + 4)):
                        curB, t_curB = Bst["cur"], Bst["t"]
                        S.op("act", lambda e, c=c, curB=curB: e.activation(out=SpB[:, c, :], in_=curB, func=AF.Copy, scale=escB[:, 2, c:c + 1]),
                             reads=[t_curB, t_escB], writes=[t_SpB[c]])
                        if c > 0:
                            k = Bst["k"]
                            Bst["k"] += 1
                            nxt, t_nxt = Spp[:, 1, k % 3, :], t_Spp[1][k % 3]
                            S.op("dve", lambda e, c=c, curB=curB, nxt=nxt: e.scalar_tensor_tensor(
                                out=nxt, in0=curB, scalar=escB[:, 0, c:c + 1], in1=pb[pbk][:, (c % 4) * 128:(c % 4 + 1) * 128],
                                op0=ALU.mult, op1=ALU.add), reads=[t_pb[pbk], t_escB, t_curB], writes=[t_nxt])
                            Bst["cur"], Bst["t"] = nxt, t_nxt
                blkst = {}

                def A_pe_e1(blk, pend):
                    tk0 = blk * 512
                    th = [t_hTo[blk]]
                    d = {}
                    for g in (2, 1, 3, 0):
                        inproj_fm(g, lambda j, g=g: w[:, g, j, :], hT_own, tk0, 512, tw, th)
                    d["sgA"], d["t_sgA"] = xr_sgA.get(); d["sgB"], d["t_sgB"] = xr_sgB.get()
                    d["qf"], d["t_qf"] = xr_qf.get()
                    sgg, t_sgg = scr.get()
                    sgA, sgB, qf = d["sgA"], d["sgB"], d["qf"]
                    S.op("act", lambda e: e.activation(out=sgB[:, 0:512], in_=pb[2][:, :], func=AF.Sigmoid, scale=-1.0),
                         reads=[t_pb[2]], writes=[d["t_sgB"]])
                    S.op("act", lambda e: e.activation(out=sgA[:, 0:512], in_=pb[1][:, :], func=AF.Sigmoid, scale=-1.0),
                         reads=[t_pb[1]], writes=[d["t_sgA"]])
                    S.op("act", lambda e: e.activation(out=sgg[:, 0:512], in_=pb[3][:, :], func=AF.Sigmoid),
                         reads=[t_pb[3]], writes=[t_sgg])
                    S.op("act", lambda e: e.activation(out=qf[:, 0:512], in_=pb[0][:, :], func=AF.Copy),
                         reads=[t_pb[0]], writes=[d["t_qf"]])
                    S.op("dve", lambda e: e.tensor_tensor(out=silu_g[:, tk0:tk0 + 512], in0=pb[3][:, :], in1=sgg[:, 0:512], op=ALU.mult),
                         reads=[t_pb[3], t_sgg], writes=[t_sil[blk]])
                    vproj(lambda j: w[:, 4, j, :], hT_own, tk0, 4, vh, blk * 4, tw, th, t_v)
                    if pend is not None:
                        B_bulk(pend)
                    blkst[blk] = d

                def A_e2(blk):
                    d = blkst[blk]
                    sgA, sgB = d["sgA"], d["sgB"]
                    lfA, t_lfA = scr.get(); lfB, t_lfB = scr.get()
                    d["bbA"], d["t_bbA"] = xr_bbA.get(); d["bbB"], d["t_bbB"] = xr_bbB.get()
                    bbA, bbB, t_bbA, t_bbB = d["bbA"], d["bbB"], d["t_bbA"], d["t_bbB"]
                    S.op("act", lambda e: e.activation(out=lfB[:, 1:513], in_=sgB[:, 0:512], func=AF.Ln,
                                                       scale=noml[:, 1, hd:hd + 1], bias=1.0), reads=[d["t_sgB"], t_lb], writes=[t_lfB])
                    S.op("act", lambda e: e.activation(out=lfA[:, 1:513], in_=sgA[:, 0:512], func=AF.Ln,
                                                       scale=noml[:, 0, hd:hd + 1], bias=1.0), reads=[d["t_sgA"], t_lb], writes=[t_lfA])
                    S.op("dve", lambda e: e.tensor_tensor_scan(out=bbB[:, 0:512], data0=lfB[:, 0:512], data1=scanmask[:, :],
                                                               initial=0.0, op0=ALU.add, op1=ALU.mult),
                         reads=[t_lfB, t_const], writes=[t_bbB])
                    S.op("dve", lambda e: e.tensor_tensor_scan(out=bbA[:, 0:512], data0=scanmask[:, :], data1=lfA[:, 1:513],
                                                               initial=0.0, op0=ALU.mult, op1=ALU.add),
                         reads=[t_lfA, t_const], writes=[t_bbA])
                    scA, t_scA = scs.get(); scB, t_scB = scs.get()
                    d["scA"], d["t_scA"], d["scB"], d["t_scB"] = scA, t_scA, scB, t_scB
                    a3 = bbA[:, 0:512].rearrange("p (c t) -> p c t", t=128)
                    b3 = bbB[:, 0:512].rearrange("p (c t) -> p c t", t=128)
                    lfB3 = lfB[:, 1:513].rearrange("p (c t) -> p c t", t=128)
                    sA3 = scA[:, 0:12].rearrange("p (a b) -> p a b", b=4)
                    sB3 = scB[:, 0:12].rearrange("p (a b) -> p a b", b=4)
                    S.op("dve", lambda e: e.tensor_tensor(out=sB3[:, 0, :], in0=b3[:, :, 127], in1=lfB3[:, :, 127], op=ALU.add),
                         reads=[t_bbB, t_lfB], writes=[t_scB])
                    S.op("dve", lambda e: e.tensor_copy(out=sB3[:, 1, :], in_=b3[:, :, 64]), reads=[t_bbB], writes=[t_scB])
                    S.op("dve", lambda e: e.tensor_tensor(out=sB3[:, 2, :], in0=sB3[:, 0, :], in1=b3[:, :, 64], op=ALU.subtract),
                         reads=[t_bbB, t_scB], writes=[t_scB])
                    S.op("dve", lambda e: e.tensor_copy(out=sA3[:, 0, :], in_=a3[:, :, 127]), reads=[t_bbA], writes=[t_scA])
                    S.op("dve", lambda e: e.tensor_tensor(out=sA3[:, 1, :], in0=a3[:, :, 127], in1=a3[:, :, 63], op=ALU.subtract),
                         reads=[t_bbA], writes=[t_scA])
                    S.op("dve", lambda e: e.tensor_copy(out=sA3[:, 2, :], in_=a3[:, :, 63]), reads=[t_bbA], writes=[t_scA])
                    S.op("pool", lambda e: e.tensor_tensor(
                        out=b3, in0=b3, in1=scB[:, 4:8].rearrange("p (c o) -> p c o", o=1).to_broadcast([128, 4, 128]), op=ALU.subtract),
                        reads=[t_bbB, t_scB], writes=[t_bbB])
                    S.op("pool", lambda e: e.tensor_tensor(
                        out=a3, in0=a3, in1=scA[:, 8:12].rearrange("p (c o) -> p c o", o=1).to_broadcast([128, 4, 128]), op=ALU.subtract),
                        reads=[t_bbA, t_scA], writes=[t_bbA])

                def A_e3(blk):
                    d = blkst[blk]
                    tk0 = blk * 512
                    c0 = blk * 4
                    sgA, sgB, qf, bbA, bbB = d["sgA"], d["sgB"], d["qf"], d["bbA"], d["bbB"]
                    t_bbA, t_bbB = d["t_bbA"], d["t_bbB"]
                    sA3 = d["scA"][:, 0:12].rearrange("p (a b) -> p a b", b=4)
                    sB3 = d["scB"][:, 0:12].rearrange("p (a b) -> p a b", b=4)
                    S.op("act", lambda e: e.activation(out=escB[:, :, c0:c0 + 4], in_=sB3, func=AF.Exp), reads=[d["t_scB"]], writes=[t_escB])
                    S.op("act", lambda e: e.activation(out=escA[:, :, c0:c0 + 4], in_=sA3, func=AF.Exp), reads=[d["t_scA"]], writes=[t_escA])
                    ekA, t_ekA = scr.get(); ekB, t_ekB = scr.get()
                    S.op("act", lambda e: e.activation(out=ekB[:, 0:512], in_=bbB[:, 0:512], func=AF.Exp), reads=[t_bbB], writes=[t_ekB])
                    S.op("act", lambda e: e.activation(out=ekA[:, 0:512], in_=bbA[:, 0:512], func=AF.Exp, scale=-1.0), reads=[t_bbA], writes=[t_ekA])
                    S.op("dve", lambda e: e.scalar_tensor_tensor(
                        out=Q[:, 3, tk0:tk0 + 512], in0=sgB[:, 0:512], scalar=oml[:, 1, hd:hd + 1], in1=ekB[:, 0:512], op0=ALU.mult, op1=ALU.mult),
                        reads=[d["t_sgB"], t_ekB, t_lb], writes=[TQ[3][blk]])
                    S.op("dve", lambda e: e.scalar_tensor_tensor(
                        out=Q[:, 2, tk0:tk0 + 512], in0=sgA[:, 0:512], scalar=oml[:, 0, hd:hd + 1], in1=ekA[:, 0:512], op0=ALU.mult, op1=ALU.mult),
                        reads=[d["t_sgA"], t_ekA, t_lb], writes=[TQ[2][blk]])
                    for (src, dst, esc, t_esc) in ((3, 5, escB, t_escB), (2, 4, escA, t_escA)):
                        S.op("pool", lambda e, src=src, dst=dst, esc=esc: e.tensor_tensor(
                            out=Q[:, dst, tk0:tk0 + 512].rearrange("p (c t) -> p c t", t=128),
                            in0=Q[:, src, tk0:tk0 + 512].rearrange("p (c t) -> p c t", t=128),
                            in1=esc[:, 1, c0:c0 + 4].rearrange("p (c o) -> p c o", o=1).to_broadcast([128, 4, 128]), op=ALU.mult),
                            reads=[TQ[src][blk], t_esc], writes=[TQ[dst][blk]])

                def A_q(blk):
                    d = blkst[blk]
                    tk0 = blk * 512
                    qf, bbA, bbB = d["qf"], d["bbA"], d["bbB"]
                    eqA, t_eqA = scr.get(); eqB, t_eqB = scr.get()
                    S.op("act", lambda e: e.activation(out=eqA[:, 0:512], in_=bbA[:, 0:512], func=AF.Exp), reads=[d["t_bbA"]], writes=[t_eqA])
                    S.op("act", lambda e: e.activation(out=eqB[:, 0:512], in_=bbB[:, 0:512], func=AF.Exp, scale=-1.0), reads=[d["t_bbB"]], writes=[t_eqB])
                    S.op("pool", lambda e: e.tensor_tensor(out=Q[:, 0, tk0:tk0 + 512], in0=qf[:, 0:512], in1=eqA[:, 0:512], op=ALU.mult),
                         reads=[d["t_qf"], t_eqA], writes=[TQ[0][blk]])
                    S.op("pool", lambda e: e.tensor_tensor(out=Q[:, 1, tk0:tk0 + 512], in0=qf[:, 0:512], in1=eqB[:, 0:512], op=ALU.mult),
                         reads=[d["t_qf"], t_eqB], writes=[TQ[1][blk]])

                bq = []
                prev_blk = None
                for blk in reversed(range(NBO)):
                    A_pe_e1(blk, None)
                    if prev_blk is not None:
                        A_e3(prev_blk)
                        bq.append(prev_blk)
                    A_e2(blk)
                    prev_blk = blk
                for k, cgp in enumerate(bq):
                    B_bulk(cgp, k)
                A_e3(prev_blk)
                for cgp in bq:
                    B_stage(cgp, with_bulk=False)
                B_stage(prev_blk)
                A_q(0)
                for cg in range(min(2, NG)):
                    tb, t_tb = (pbt, t_pbt) if cg % 2 == 0 else (pb[4][:, :].bitcast(BF16), t_pb[4])
                    bulk_P(4, cg, cg % 2, tb, t_tb, vh, t_v)
                later = []
                curA = [Sst[:, siA, :], t_S[siA]]

                def emit_AT(c):
                    blk = c // 4
                    cs = slice(c * 128, (c + 1) * 128)
                    ab = 5 if c % 2 == 0 else 2
                    am, t_am = atm[c % 2], t_atm[c % 2]
                    S.op("pe", lambda e: e.matmul(pb[ab][:, 0:128], lhsT=Q[:, 2, cs], rhs=Q[:, 0, cs], start=True, stop=True),
                         reads=[TQ[2][blk], TQ[0][blk]], writes=[t_pb[ab]])
                    S.op("pe", lambda e: e.matmul(pb[ab][:, 128:256], lhsT=Q[:, 3, cs], rhs=Q[:, 1, cs], start=True, stop=True),
                         reads=[TQ[3][blk], TQ[1][blk]], writes=[t_pb[ab]])
                    S.op("dve", lambda e: e.tensor_tensor(out=am[:].rearrange("p a b -> p (a b)"), in0=pb[ab][:, 0:256],
                                                          in1=mask2[:].rearrange("p a b -> p (a b)"), op=ALU.mult),
                         reads=[t_pb[ab], t_const], writes=[t_am])

                def emit_o(c):
                    blk, cq = c // 4, c % 4
                    cs = slice(c * 128, (c + 1) * 128)
                    ob = 6 if blk % 2 == 0 else 3
                    am, t_am = atm[c % 2], t_atm[c % 2]
                    spa, t_spa = SpA[c % 2], t_SpA[c % 2]
                    cA, t_cA = curA
                    S.op("act", lambda e: e.activation(out=spa[:], in_=cA, func=AF.Copy, scale=escA[:, 2, c:c + 1]),
                         reads=[t_cA, t_escA], writes=[t_spa])
                    if c < NCO - 1:
                        nxt, t_nxt = Spp[:, 0, c % 3, :], t_Spp[0][c % 3]
                        pbk = (c // 4) % 2
                        S.op("dve", lambda e: e.scalar_tensor_tensor(out=nxt, in0=cA, scalar=escA[:, 0, c:c + 1],
                                                                     in1=pb[pbk][:, (c % 4) * 128:(c % 4 + 1) * 128], op0=ALU.mult, op1=ALU.add),
                             reads=[t_pb[pbk], t_escA, t_cA], writes=[t_nxt])
                        curA[0], curA[1] = nxt, t_nxt
                    oslc = pb[ob][:, cq * 128:(cq + 1) * 128]
                    S.op("pe", lambda e: e.matmul(oslc, lhsT=vh[:, c, :], rhs=am[:, 0, :], start=True, stop=False),
                         reads=[t_v, t_am], writes=[t_pb[ob]])
                    S.op("pe", lambda e: e.matmul(oslc, lhsT=vh[:, c, :], rhs=am[:, 1, :], start=False, stop=False),
                         reads=[t_v, t_am], writes=[t_pb[ob]])
                    S.op("pe", lambda e: e.matmul(oslc, lhsT=spa[:], rhs=Q[:, 0, cs], start=False, stop=False),
                         reads=[t_spa, TQ[0][blk]], writes=[t_pb[ob]])
                    S.op("pe", lambda e: e.matmul(oslc, lhsT=SpB[:, c, :], rhs=Q[:, 1, cs], start=False, stop=True),
                         reads=[t_SpB[c], TQ[1][blk]], writes=[t_pb[ob]])
                    if cq == 3:
                        tk0 = blk * 512
                        sq, t_sq = sqb[blk % 2], t_sqb[blk % 2]
                        rs, t_rs = scr.get(); t1, t_t1 = scr.get()
                        S.op("act", lambda e: e.activation(out=sq[:, 0:512], in_=pb[ob][:, :], func=AF.Square),
                             reads=[t_pb[ob]], writes=[t_sq])

                        def p2a():
                            S.op("pe", lambda e: e.matmul(pb[4][:, :], lhsT=ones_b[:], rhs=sq[:, 0:512], start=True, stop=True),
                                 reads=[t_sq, t_onesb], writes=[t_pb[4]])

                        def p2b():
                            S.op("act", lambda e: e.activation(out=rs[:, 0:512], in_=pb[4][:, :], func=AF.Ln, scale=1.0 / 128, bias=EPS),
                                 reads=[t_pb[4]], writes=[t_rs])
                            S.op("act", lambda e: e.activation(out=rs[:, 0:512], in_=rs[:, 0:512], func=AF.Exp, scale=-0.5),
                                 reads=[t_rs], writes=[t_rs])

                        def p2c():
                            S.op("dve", lambda e: e.scalar_tensor_tensor(out=t1[:, 0:512], in0=pb[ob][:, :], scalar=onw[:, 0:1],
                                                                         in1=rs[:, 0:512], op0=ALU.mult, op1=ALU.mult),
                                 reads=[t_pb[ob], t_rs, t_misc], writes=[t_t1])
                            S.op("pool", lambda e: e.tensor_tensor(out=ymixT[:, hd, tk0:tk0 + 512], in0=t1[:, 0:512],
                                                                   in1=silu_g[:, tk0:tk0 + 512], op=ALU.mult),
                                 reads=[t_t1, t_sil[blk]], writes=[t_ym[hd][blk]])
                        later.append([1, p2a]); later.append([2, p2b]); later.append([3, p2c])

                emit_AT(0)
                for c in range(NCO):
                    if c % 4 == 0 and c // 4 + 1 < NBO:
                        A_q(c // 4 + 1)
                    if c + 1 < NCO:
                        emit_AT(c + 1)
                    emit_o(c)
                    if c % 4 == 3 and c // 4 + 2 < NG:
                        cg2 = c // 4 + 2
                        bulk_P(4, cg2, cg2 % 2, pbt, t_pbt, vh, t_v)
                    for it in later:
                        it[0] -= 1
                    for it in [it for it in later if it[0] < 0]:
                        it[1]()
                        later.remove(it)
                for it in later:
                    it[1]()
        S.fence()
        for cc in range(NCC):
            for t in t_ym[NH + cc]:
                t.r = list(S.cur_fence)

        wout = sb(es, "wout", [128, NM, D], BF16); t_wout = S.tok("wout")
        with ExitStack() as ph:
            wc = [sb(ph, f"wc{i}", [128, 4, KD, 128], BF16) for i in range(2)]
            t_wc = S.toks("wc", 2)
            scr = Pool_(S, nc, ph, "scr3_", 12, [128, 516], F32)

            def load_wc(cc):
                S.dma("pool", wc[cc % 2][:].rearrange("p a b c -> p (a b c)"), wcv_d[cc, :, :], writes=[t_wc[cc % 2]],
                      max_dma_last_dim=4096)
            load_wc(0)
            for cc in range(NCC):
                if cc + 1 < NCC:
                    load_wc(cc + 1)
                if cc == 0:
                    S.dma("pool", wout[:].rearrange("p a b -> p (a b)"), wout_d[:, :], writes=[t_wout], max_dma_last_dim=4096)
                w, tw = wc[cc % 2], t_wc[cc % 2]
                for blk in range(NBO):
                    tk0 = blk * 512
                    th = [t_hTo[blk]]
                    if (cc * NBO + blk) % 2 == 0:
                        cb = [(pb[i][:, :], t_pb[i]) for i in range(4)]
                    else:
                        cb = [(pb[4][:, :], t_pb[4]), (pb[5][:, :], t_pb[5]), (pb[6][:, :], t_pb[6]), (pbt[:, :].bitcast(F32), t_pbt)]
                    for g in range(4):
                        for j in range(KD):
                            S.op("pe", lambda e, j=j, g=g: e.matmul(cb[g][0], lhsT=w[:, g, j, :], rhs=hT_own[:, j, tk0:tk0 + 512],
                                                                    start=(j == 0), stop=(j == KD - 1)),
                                 reads=[tw] + th, writes=[cb[g][1]])
                    hs, t_hs = scr.get(); u, t_u = scr.get(); acc, t_acc = scr.get(); sgg, t_sgg = scr.get()
                    t1, t_t1 = scr.get(); t2, t_t2 = scr.get()
                    S.op("act", lambda e, hs=hs: e.activation(out=hs[:, 0:512], in_=cb[0][0], func=AF.Copy), reads=[cb[0][1]], writes=[t_hs])
                    S.op("act", lambda e, sgg=sgg: e.activation(out=sgg[:, 0:512], in_=cb[3][0], func=AF.Sigmoid), reads=[cb[3][1]], writes=[t_sgg])
                    S.op("dve", lambda e, hs=hs, u=u: e.tensor_tensor(out=u[:, 0:512], in0=cb[2][0], in1=hs[:, 0:512], op=ALU.mult),
                         reads=[cb[2][1], t_hs], writes=[t_u])
                    u3 = u[:, 0:512].rearrange("p (r t) -> p r t", t=GW)
                    a3 = acc[:, 0:512].rearrange("p (r t) -> p r t", t=GW)
                    S.op("pool", lambda e, u=u, acc=acc: e.tensor_scalar(out=acc[:, 0:512], in0=u[:, 0:512], scalar1=convw[:, 1, cc:cc + 1],
                                                                        scalar2=0.0, op0=ALU.mult, op1=ALU.add),
                         reads=[t_u, t_misc], writes=[t_acc])
                    S.op("dve", lambda e, u3=u3, a3=a3: e.scalar_tensor_tensor(out=a3[:, :, 1:GW], in0=u3[:, :, 0:GW - 1], scalar=convw[:, 0, cc:cc + 1],
                                                                                in1=a3[:, :, 1:GW], op0=ALU.mult, op1=ALU.add),
                         reads=[t_u, t_misc, t_acc], writes=[t_acc])
                    S.op("dve", lambda e, u3=u3, a3=a3: e.scalar_tensor_tensor(out=a3[:, :, 0:GW - 1], in0=u3[:, :, 1:GW], scalar=convw[:, 2, cc:cc + 1],
                                                                                in1=a3[:, :, 0:GW - 1], op0=ALU.mult, op1=ALU.add),
                         reads=[t_u, t_misc, t_acc], writes=[t_acc])
                    S.op("dve", lambda e, acc=acc, t1=t1: e.tensor_tensor(out=t1[:, 0:512], in0=cb[1][0], in1=acc[:, 0:512], op=ALU.mult),
                         reads=[cb[1][1], t_acc], writes=[t_t1])
                    S.op("dve", lambda e, sgg=sgg, t2=t2: e.tensor_tensor(out=t2[:, 0:512], in0=cb[3][0], in1=sgg[:, 0:512], op=ALU.mult),
                         reads=[cb[3][1], t_sgg], writes=[t_t2])
                    S.op("dve", lambda e, t1=t1, t2=t2, tk0=tk0: e.tensor_tensor(out=ymixT[:, NH + cc, tk0:tk0 + 512], in0=t1[:, 0:512],
                                                                                 in1=t2[:, 0:512], op=ALU.mult),
                         reads=[t_t1, t_t2], writes=[t_ym[NH + cc][blk]])
        S.fence()

        with ExitStack() as ph:
            NXB = 3
            xr = [sb(ph, f"xr{i}", [128, D]) for i in range(NXB)]; t_xr = S.toks("xr", NXB)
            yt = [sb(ph, f"yt{i}", [128, D]) for i in range(NXB)]; t_yt = S.toks("yt", NXB)
            ot = [sb(ph, f"ot{i}", [128, D]) for i in range(NXB)]; t_ot = S.toks("ot", NXB)
            junk = sb(ph, "junk3", [128, D]); t_junk = S.tok("junk3")
            fs = sb(ph, "fs", [128, 3, NCO]); t_fs = S.toks("fs", NCO)
            t_out = S.toks("out", NCO)
            for i in range(NCO):
                k = i % NXB
                blk = i // 4
                S.dma("sp", xr[k][:], x_own[i * 128:(i + 1) * 128, :], writes=[t_xr[k]])
                for hf in range(NHALF):
                    bank = (i * NHALF + hf) % 4
                    for m in range(NM):
                        S.op("pe", lambda e, m=m, hf=hf, bank=bank: e.matmul(
                            pb[bank][:, 0:HALF], lhsT=ymixT[:, m, i * 128:(i + 1) * 128], rhs=wout[:, m, hf * HALF:(hf + 1) * HALF],
                            start=(m == 0), stop=(m == NM - 1)), reads=[t_ym[m][blk], t_wout], writes=[t_pb[bank]])
                    S.op("dve", lambda e, hf=hf, bank=bank, k=k: e.tensor_tensor(
                        out=yt[k][:, hf * HALF:(hf + 1) * HALF], in0=pb[bank][:, 0:HALF], in1=gate_bc[:, hf * HALF:(hf + 1) * HALF], op=ALU.mult),
                        reads=[t_pb[bank], t_gate], writes=[t_yt[k]])
                S.op("pool", lambda e, k=k: e.tensor_tensor(out=yt[k][:], in0=yt[k][:], in1=xr[k][:], op=ALU.add),
                     reads=[t_yt[k], t_xr[k]], writes=[t_yt[k]])
                S.op("act", lambda e, k=k: e.activation(out=junk[:], in_=yt[k][:], func=AF.Square, accum_out=fs[:, 0, i:i + 1]),
                     reads=[t_yt[k]], writes=[t_junk, t_fs[i]])
                S.op("act", lambda e: e.activation(out=fs[:, 1, i:i + 1], in_=fs[:, 0, i:i + 1], func=AF.Ln, scale=1.0 / D, bias=EPS),
                     reads=[t_fs[i]], writes=[t_fs[i]])
                S.op("act", lambda e: e.activation(out=fs[:, 2, i:i + 1], in_=fs[:, 1, i:i + 1], func=AF.Exp, scale=-0.5),
                     reads=[t_fs[i]], writes=[t_fs[i]])
                S.op("dve", lambda e, k=k: e.scalar_tensor_tensor(out=ot[k][:], in0=yt[k][:], scalar=fs[:, 2, i:i + 1], in1=fw_bc[:],
                                                                  op0=ALU.mult, op1=ALU.mult),
                     reads=[t_yt[k], t_fs[i], t_fw], writes=[t_ot[k]])
                S.dma("act", out_d[i * 128:(i + 1) * 128, :], ot[k][:], reads=[t_ot[k]], writes=[t_out[i]])
            S.wait_all("sp", t_out)
        build.stats = (S.n_inst, S.n_wait)
    return nc


def host_prep(cfg, inp, b, s, HGW, CW):
    D, NH, NCC, KD, NM = cfg.D, cfg.NH, cfg.NCC, cfg.KD, cfg.NM
    f = np.float32
    flip = (s == 0)
    xb = inp["x"][b]
    cx = inp["ctx"][b]
    if flip:
        xb = xb[::-1]
        cx = cx[::-1]
    m = {}
    m["x_pre"] = np.ascontiguousarray(xb[:cfg.T_PRE], f)
    m["x_own"] = np.ascontiguousarray(xb[cfg.T_PRE:], f)
    m["x_ctx"] = np.ascontiguousarray(cx, f)
    cv = np.stack([inp["c"][b].reshape(KD, 128).T, inp["c_ctx"].reshape(KD, 128).T], -1)
    m["cvec"] = np.ascontiguousarray(cv.reshape(128, KD * 2), f)
    wa = inp["w_ada"][0]
    wa = wa.reshape(KD, 128, 3, D).transpose(2, 1, 0, 3)
    m["wada"] = np.ascontiguousarray(wa.reshape(3, 128, KD * D), f)
    ba = inp["b_ada"][0].reshape(3, KD, 128)
    m["bada_fm"] = np.ascontiguousarray(ba[0:2].transpose(2, 0, 1).reshape(128, 2 * KD), f)
    m["bada_gate_bc"] = np.ascontiguousarray(np.broadcast_to(inp["b_ada"][0][2 * D:3 * D], (128, D)), f)
    m["normw_fm"] = np.ascontiguousarray(inp["norm_w"][0].reshape(KD, 128).T, f)
    lg = inp["hg_lb_logits"]
    dA, dB = (1, 0) if flip else (0, 1)
    l2 = np.stack([lg[dA], lg[dB]], 0).reshape(2, 2, NH, 128).transpose(3, 0, 1, 2)
    m["lbl"] = np.ascontiguousarray(l2.reshape(128, 4 * NH), f)
    m["onw"] = np.ascontiguousarray(inp["hg_onorm_w"][0].reshape(128, 1), f)
    cw = inp["conv_w"][0]
    if flip:
        cw = cw[::-1]
    m["convw"] = np.ascontiguousarray(cw.reshape(3, NCC, 128).transpose(2, 0, 1).reshape(128, 3 * NCC), f)
    w = inp["w_in"][0]
    o_q, o_i, o_zf, o_zb, o_g = 0, HGW, 2 * HGW, 3 * HGW, 4 * HGW
    o_zA, o_zB = (o_zb, o_zf) if flip else (o_zf, o_zb)
    oc = 5 * HGW

    def grp(off):
        return w[:, off:off + 128].reshape(KD, 128, 128).transpose(1, 0, 2)
    wpre = np.empty((NH, 128, 3, KD, 128), f)
    whd = np.empty((NH, 128, 5, KD, 128), f)
    for hd in range(NH):
        h0 = hd * 128
        wpre[hd, :, 0], wpre[hd, :, 1], wpre[hd, :, 2] = grp(o_zA + h0), grp(o_zB + h0), grp(o_i + h0)
        for gi, off in enumerate([o_q, o_zA, o_zB, o_g, o_i]):
            whd[hd, :, gi] = grp(off + h0)
    wcv = np.empty((NCC, 128, 4, KD, 128), f)
    for cc in range(NCC):
        for gi in range(4):
            wcv[cc, :, gi] = grp(oc + gi * CW + cc * 128)
    m["w_pre"] = wpre.reshape(NH, 128, 3 * KD * 128)
    m["w_hd"] = whd.reshape(NH, 128, 5 * KD * 128)
    m["w_cv"] = wcv.reshape(NCC, 128, 4 * KD * 128)
    wo = inp["w_out"][0].reshape(NM, 128, D).transpose(1, 0, 2)
    m["w_out"] = np.ascontiguousarray(wo.reshape(128, NM * D), f)
    m["fw_bc"] = np.ascontiguousarray(np.broadcast_to(inp["final_norm_w"], (128, D)), f)
    return m


def assemble(cfg, results, B, SEQ):
    out = np.empty((B, SEQ, cfg.D), np.float32)
    for b in range(B):
        for s in range(2):
            r = results[b * 2 + s]["out"]
            if s == 0:
                out[b, :cfg.T_OWN] = r[::-1]
            else:
                out[b, cfg.T_PRE:] = r
    return out


_NC_CACHE = {}


def kernel(**inputs):
    cfg = FULL
    inp = {k: np.asarray(v) for k, v in inputs.items()}
    B, SEQ, _ = inp["x"].shape
    assert B * 2 == 8 and SEQ == cfg.T_OWN + cfg.T_PRE
    if "nc" not in _NC_CACHE:
        _NC_CACHE["nc"] = build(cfg)
    nc = _NC_CACHE["nc"]
    in_maps = [host_prep(cfg, inp, b, s, 1024, 1024) for b in range(B) for s in range(2)]
    res = run_bass_kernel_spmd(nc, in_maps, core_ids=list(range(8)))
    return assemble(cfg, res.results, B, SEQ)
```

```python
import numpy as np
from contextlib import ExitStack
import concourse.bass as bass
import concourse.mybir as mybir
from concourse.bass_utils import run_bass_kernel_spmd

F32 = mybir.dt.float32
BF16 = mybir.dt.bfloat16
AF = mybir.ActivationFunctionType
ALU = mybir.AluOpType
EPS = 1e-6


class Tok:
    __slots__ = ("name", "w", "r")

    def __init__(self, name="", fence=None):
        self.name = name
        self.w = {}
        self.r = list(fence) if fence else []


class Sched:
    ENG = ("pe", "act", "dve", "pool", "sp")

    def __init__(self, nc, es, n_dma_sems=32, n_sw_sems=40):
        self.nc = nc
        self.h = {"pe": nc.tensor, "act": nc.scalar, "dve": nc.vector, "pool": nc.gpsimd, "sp": nc.sync}
        self.sem, self.count, self.clock, self.snap = {}, {}, {}, {}
        for e in self.ENG:
            self.sem[e] = es.enter_context(nc.semaphore(f"sem_{e}"))
            self.count[e] = 0
            self.clock[e] = {}
        self.dma_sems = []
        for i in range(n_dma_sems):
            tl = f"dma{i}"
            self.sem[tl] = es.enter_context(nc.semaphore(f"sem_{tl}"))
            self.count[tl] = 0
            self.dma_sems.append(tl)
        self.sw_sems = []
        for i in range(n_sw_sems):
            tl = f"dmasw{i}"
            self.sem[tl] = es.enter_context(nc.semaphore(f"sem_{tl}"))
            self.count[tl] = 0
            self.sw_sems.append(tl)
        self.sw_next = 0
        self.dma_rr = 0
        self.n_wait = 0
        self.n_inst = 0
        self.cur_fence = []

    def tok(self, name=""):
        return Tok(name, self.cur_fence)

    def toks(self, name, n):
        return [self.tok(f"{name}{i}") for i in range(n)]

    def fence(self):
        self.cur_fence = [(tl, c) for tl, c in self.count.items() if c > 0]

    def _wait(self, eng, ev):
        tl, cnt = ev
        ck = self.clock[eng]
        if ck.get(tl, 0) >= cnt:
            return
        mult = 16 if tl.startswith("dma") else 1
        self.h[eng].wait_ge(self.sem[tl], cnt * mult)
        self.n_wait += 1
        sn = self.snap.get(ev)
        if sn:
            for k, v in sn.items():
                if ck.get(k, 0) < v:
                    ck[k] = v
        ck[tl] = cnt

    def _deps(self, reads, writes):
        best = {}

        def add(tl, cnt):
            if best.get(tl, 0) < cnt:
                best[tl] = cnt
        for t in reads:
            for tl, cnt in t.w.items():
                add(tl, cnt)
        for t in writes:
            for tl, cnt in t.w.items():
                add(tl, cnt)
            for tl, cnt in t.r:
                add(tl, cnt)
        return best

    def _commit(self, me, eng, reads, writes):
        self.snap[me] = dict(self.clock[eng])
        for t in reads:
            t.r.append(me)
        for t in writes:
            t.w[me[0]] = me[1]
            t.r = []

    def op(self, eng, fn, reads=(), writes=()):
        for tl, cnt in self._deps(reads, writes).items():
            if tl == eng and eng == "pe":
                continue
            self._wait(eng, (tl, cnt))
        inst = fn(self.h[eng])
        self.count[eng] += 1
        inst.then_inc(self.sem[eng], 1)
        self.n_inst += 1
        me = (eng, self.count[eng])
        self._commit(me, eng, reads, writes)
        return me

    def dma(self, eng, out, in_, reads=(), writes=(), **kw):
        for tl, cnt in self._deps(reads, writes).items():
            self._wait(eng, (tl, cnt))
        if eng == "pool":
            tl = self.sw_sems[self.sw_next]
            self.sw_next += 1
        else:
            tl = self.dma_sems[self.dma_rr % len(self.dma_sems)]
            self.dma_rr += 1
            if self.count[tl] > 0:
                self._wait(eng, (tl, self.count[tl]))
        inst = self.h[eng].dma_start(out=out, in_=in_, **kw)
        self.count[tl] += 1
        inst.then_inc(self.sem[tl], 16)
        self.n_inst += 1
        me = (tl, self.count[tl])
        self._commit(me, eng, reads, writes)
        return me

    def wait_all(self, eng, tokens):
        best = {}
        for t in tokens:
            for tl, cnt in list(t.w.items()) + list(t.r):
                if best.get(tl, 0) < cnt:
                    best[tl] = cnt
        for tl, cnt in best.items():
            self._wait(eng, (tl, cnt))


class Pool_:
    def __init__(self, S, nc, es, name, n, shape, dtype):
        self.tiles = [es.enter_context(nc.sbuf_tensor(f"{name}{i}", shape, dtype)) for i in range(n)]
        self.toks = [S.tok(f"{name}{i}") for i in range(n)]
        self.i = 0

    def get(self):
        k = self.i % len(self.tiles)
        self.i += 1
        return self.tiles[k], self.toks[k]


class Cfg:
    def __init__(self, D=1024, NH=8, NCC=8, T_OWN=2048, T_PRE=2048, T_CTX=256, GRID_W=64):
        self.D, self.NH, self.NCC = D, NH, NCC
        self.T_OWN, self.T_PRE, self.T_CTX, self.GRID_W = T_OWN, T_PRE, T_CTX, GRID_W
        self.KD = D // 128
        self.NM = NH + NCC
        self.HALF = min(512, D)
        self.NHALF = D // self.HALF


FULL = Cfg()


def build(cfg):
    D, NH, NCC, KD, NM = cfg.D, cfg.NH, cfg.NCC, cfg.KD, cfg.NM
    T_OWN, T_PRE, T_CTX = cfg.T_OWN, cfg.T_PRE, cfg.T_CTX
    HALF, NHALF = cfg.HALF, cfg.NHALF
    NBO, NBP = T_OWN // 512, T_PRE // 512
    NCO, NCP, NCX = T_OWN // 128, T_PRE // 128, T_CTX // 128
    GW = cfg.GRID_W

    nc = bass.Bass("TRN2", target_bir_lowering=False)
    dt = lambda n, s, k="ExternalInput": nc.dram_tensor(n, s, F32, kind=k).ap()
    x_pre, x_own, x_ctx = dt("x_pre", [T_PRE, D]), dt("x_own", [T_OWN, D]), dt("x_ctx", [T_CTX, D])
    cvec_d = dt("cvec", [128, KD * 2])
    wada_d = dt("wada", [3, 128, KD * D])
    badafm_d = dt("bada_fm", [128, 2 * KD])
    badag_d = dt("bada_gate_bc", [128, D])
    normw_d = dt("normw_fm", [128, KD])
    lbl_d = dt("lbl", [128, 4 * NH])
    onw_d = dt("onw", [128, 1])
    convw_d = dt("convw", [128, 3 * NCC])
    wpre_d = dt("w_pre", [NH, 128, 3 * KD * 128])
    whd_d = dt("w_hd", [NH, 128, 5 * KD * 128])
    wcv_d = dt("w_cv", [NCC, 128, 4 * KD * 128])
    wout_d = dt("w_out", [128, NM * D])
    fw_d = dt("fw_bc", [128, D])
    out_d = dt("out", [T_OWN, D], "ExternalOutput")

    with ExitStack() as es:
        S = Sched(nc, es)
        sb = lambda st, n, s, d=F32: st.enter_context(nc.sbuf_tensor("s_" + n, s, d))

        ident_f = sb(es, "ident_f", [128, 128]); ident_b = sb(es, "ident_b", [128, 128], BF16)
        ones_f = sb(es, "ones_f", [128, 128])
        mask2 = sb(es, "mask2", [128, 2, 128])
        maskA = mask2[:, 0, :]; maskB = mask2[:, 1, :]
        scanmask = sb(es, "scanmask", [128, 512])
        hT_own = sb(es, "hT_own", [128, KD, T_OWN], BF16)
        Sst = sb(es, "Sst", [128, 2 * NH, 128])
        gate_bc = sb(es, "gate_bc", [128, D]); fw_bc = sb(es, "fw_bc", [128, D])
        modsb = sb(es, "modsb", [128, 2 * KD, 2]); gvec = sb(es, "gvec", [128, KD, 2])
        lbl = sb(es, "lbl", [128, 2, 2, NH]); lbd = sb(es, "lbd", [128, 2, NH])
        oml = sb(es, "oml", [128, 2, NH]); noml = sb(es, "noml", [128, 2, NH])
        onw = sb(es, "onw", [128, 1]); convw = sb(es, "convw", [128, 3, NCC])
        normw = sb(es, "normw", [128, KD]); badafm = sb(es, "badafm", [128, 2 * KD])
        t_const = S.tok("const")
        t_hTo = S.toks("hTo", NBO)
        t_hTc = S.tok("hTc")
        t_S = S.toks("S", 2 * NH)
        t_gate, t_fw, t_mod, t_lb, t_misc = S.tok("gate"), S.tok("fw"), S.tok("mod"), S.tok("lb"), S.tok("misc")

        pb = [es.enter_context(nc.psum_tensor(f"pb{i}", [128, 512], F32)) for i in range(7)]
        pbt = es.enter_context(nc.psum_tensor("pbt", [128, 1024], BF16))
        t_pb = [S.tok(f"pb{i}") for i in range(7)]
        t_pb4q = [t_pb[4]] * 4
        t_pb5q = [t_pb[5]] * 4

        P = "pool"
        S.op(P, lambda e: e.memset(ident_f[:], 0.0), writes=[t_const])
        S.op(P, lambda e: e.affine_select(out=ident_f[:], in_=ident_f[:], pattern=[[-1, 128]], compare_op=ALU.not_equal,
                                          fill=1.0, base=0, channel_multiplier=1), reads=[t_const], writes=[t_const])
        S.op(P, lambda e: e.tensor_copy(out=ident_b[:], in_=ident_f[:]), reads=[t_const], writes=[t_const])
        S.op(P, lambda e: e.memset(ones_f[:], 1.0), writes=[t_const])
        S.op(P, lambda e: e.memset(maskA, 1.0), writes=[t_const])
        S.op(P, lambda e: e.affine_select(out=maskA, in_=maskA, pattern=[[1, 128]], compare_op=ALU.is_ge,
                                          fill=0.0, base=0, channel_multiplier=-1), reads=[t_const], writes=[t_const])
        S.op(P, lambda e: e.memset(maskB, 1.0), writes=[t_const])
        S.op(P, lambda e: e.affine_select(out=maskB, in_=maskB, pattern=[[-1, 128]], compare_op=ALU.is_ge,
                                          fill=0.0, base=0, channel_multiplier=1), reads=[t_const], writes=[t_const])
        S.op(P, lambda e: e.memset(scanmask[:], 1.0), writes=[t_const])
        S.op(P, lambda e: e.memset(scanmask[:].rearrange("p (c t) -> p c t", t=128)[:, :, 0:1], 0.0), writes=[t_const])
        S.op(P, lambda e: e.memset(Sst[:], 0.0), writes=t_S)

        S.dma("sp", lbl[:].rearrange("p a b c -> p (a b c)"), lbl_d[:, :], writes=[t_lb])
        S.dma("sp", onw[:], onw_d[:, :], writes=[t_misc])
        S.dma("sp", convw[:].rearrange("p a b -> p (a b)"), convw_d[:, :], writes=[t_misc])
        S.dma("sp", normw[:], normw_d[:, :], writes=[t_misc])
        S.dma("sp", badafm[:], badafm_d[:, :], writes=[t_misc])
        S.dma("sp", fw_bc[:], fw_d[:, :], writes=[t_fw])
        S.op("dve", lambda e: e.tensor_tensor(out=lbd[:], in0=lbl[:, :, 0, :], in1=lbl[:, :, 1, :], op=ALU.subtract),
             reads=[t_lb], writes=[t_lb])
        S.op("act", lambda e: e.activation(out=oml[:], in_=lbd[:], func=AF.Sigmoid, scale=-1.0), reads=[t_lb], writes=[t_lb])
        S.op("dve", lambda e: e.tensor_scalar(out=noml[:], in0=oml[:], scalar1=-1.0, scalar2=None, op0=ALU.mult),
             reads=[t_lb], writes=[t_lb])

        scs = Pool_(S, nc, es, "scs_", 4, [128, 12], F32)
        st1 = ExitStack()
        hT_pre = sb(st1, "hT_pre", [128, KD, T_PRE], BF16)
        hT_ctx = sb(st1, "hT_ctx", [128, KD, T_CTX], BF16)
        t_hTp = S.toks("hTp", NBP)

        with ExitStack() as ph:
            wada_sb = [sb(ph, f"wada{i}", [128, KD, D]) for i in range(2)]
            t_wada = S.toks("wada", 2)
            cv = sb(ph, "cv", [128, KD, 2]); csg = sb(ph, "csg", [128, KD, 2]); sc2 = sb(ph, "sc2", [128, KD, 2])
            screp = sb(ph, "screp", [128, KD, 128])
            t_c = S.tok("c")
            NXT = 8
            xt = [sb(ph, f"xt{i}", [128, D]) for i in range(NXT)]
            t_xt = S.toks("xt", NXT)
            junk = sb(ph, "junk", [128, D]); t_junk = S.tok("junk")
            NT_ALL = NCP + NCX + NCO
            ssq = sb(ph, "ssq", [128, NT_ALL]); lnv = sb(ph, "lnv", [128, NT_ALL]); rstd = sb(ph, "rstd", [128, NT_ALL])
            t_ss = S.toks("ss", NT_ALL)

            S.dma("sp", cv[:].rearrange("p a b -> p (a b)"), cvec_d[:, :], writes=[t_c])
            S.op("act", lambda e: e.activation(out=csg[:], in_=cv[:], func=AF.Sigmoid), reads=[t_c], writes=[t_c])
            S.op("dve", lambda e: e.tensor_tensor(out=sc2[:], in0=cv[:], in1=csg[:], op=ALU.mult), reads=[t_c], writes=[t_c])
            S.op("dve", lambda e: e.tensor_copy(out=screp[:], in_=sc2[:, :, 0:1].to_broadcast([128, KD, 128])),
                 reads=[t_c], writes=[t_c])
            for kind in range(3):
                wt, tw = wada_sb[kind % 2], t_wada[kind % 2]
                S.dma("act" if kind % 2 else "sp", wt[:].rearrange("p a b -> p (a b)"), wada_d[kind, :, :], writes=[tw])
                if kind < 2:
                    for jo in range(KD):
                        col = (kind * KD + jo) * 2
                        for ji in range(KD):
                            S.op("pe", lambda e, ji=ji, jo=jo, col=col, wt=wt: e.matmul(
                                pb[6][:, col:col + 2], lhsT=wt[:, ji, jo * 128:(jo + 1) * 128], rhs=sc2[:, ji, :],
                                start=(ji == 0), stop=(ji == KD - 1)), reads=[tw, t_c], writes=[t_pb[6]])
                else:
                    for hf in range(NHALF):
                        for ji in range(KD):
                            S.op("pe", lambda e, ji=ji, hf=hf, wt=wt: e.matmul(
                                pb[4 + hf][:, 0:HALF], lhsT=screp[:, ji, :], rhs=wt[:, ji, hf * HALF:(hf + 1) * HALF],
                                start=(ji == 0), stop=(ji == KD - 1)), reads=[tw, t_c], writes=[t_pb[4 + hf]])
            S.op("dve", lambda e: e.tensor_tensor(
                out=modsb[:], in0=pb[6][:, 0:4 * KD].rearrange("p (a b) -> p a b", b=2),
                in1=badafm[:].rearrange("p (a o) -> p a o", o=1).to_broadcast([128, 2 * KD, 2]), op=ALU.add),
                reads=[t_pb[6], t_misc], writes=[t_mod])
            S.op("dve", lambda e: e.scalar_tensor_tensor(
                out=gvec[:], in0=modsb[:, KD:2 * KD, :], scalar=1.0,
                in1=normw[:].rearrange("p (a o) -> p a o", o=1).to_broadcast([128, KD, 2]), op0=ALU.add, op1=ALU.mult),
                reads=[t_mod, t_misc], writes=[t_mod])
            gb = sb(ph, "gb", [128, D]); t_gb = S.tok("gb")
            S.dma("sp", gb[:], badag_d[:, :], writes=[t_gb])
            for hf in range(NHALF):
                S.op("dve", lambda e, hf=hf: e.tensor_tensor(out=gate_bc[:, hf * HALF:(hf + 1) * HALF], in0=pb[4 + hf][:, 0:HALF],
                                                             in1=gb[:, hf * HALF:(hf + 1) * HALF], op=ALU.add),
                     reads=[t_pb[4 + hf], t_gb], writes=[t_gate])

            segs = [(x_pre, hT_pre, t_hTp, T_PRE, 0, 0), (x_ctx, hT_ctx, [t_hTc], T_CTX, 1, NCP),
                    (x_own, hT_own, t_hTo, T_OWN, 0, NCP + NCX)]
            xi = 0
            evq = 0
            for (xd, hT, thT, TT, which, tbase) in segs:
                for g0 in range(0, TT // 128, 4):
                    G = min(4, TT // 128 - g0)
                    bufs = []
                    for i in range(G):
                        k = xi % NXT
                        xi += 1
                        ti = g0 + i
                        gi = tbase + ti
                        S.dma("sp", xt[k][:], xd[ti * 128:(ti + 1) * 128, :], writes=[t_xt[k]])
                        S.op("act", lambda e, k=k, gi=gi: e.activation(out=junk[:], in_=xt[k][:], func=AF.Square,
                                                                       accum_out=ssq[:, gi:gi + 1]),
                             reads=[t_xt[k]], writes=[t_junk, t_ss[gi]])
                        S.op("act", lambda e, gi=gi: e.activation(out=lnv[:, gi:gi + 1], in_=ssq[:, gi:gi + 1], func=AF.Ln,
                                                                  scale=1.0 / D, bias=EPS), reads=[t_ss[gi]], writes=[t_ss[gi]])
                        S.op("act", lambda e, gi=gi: e.activation(out=rstd[:, gi:gi + 1], in_=lnv[:, gi:gi + 1], func=AF.Exp,
                                                                  scale=-0.5), reads=[t_ss[gi]], writes=[t_ss[gi]])
                        S.op("dve", lambda e, k=k, gi=gi: e.tensor_scalar(out=xt[k][:], in0=xt[k][:], scalar1=rstd[:, gi:gi + 1],
                                                                          scalar2=None, op0=ALU.mult),
                             reads=[t_xt[k], t_ss[gi]], writes=[t_xt[k]])
                        bufs.append(k)
                    blk = g0 // 4
                    for j in range(KD):
                        bank = j % 2
                        for i, k in enumerate(bufs):
                            S.op("pe", lambda e, i=i, k=k, j=j, bank=bank: e.transpose(
                                out=pb[bank][:, i * 128:(i + 1) * 128], in_=xt[k][:, j * 128:(j + 1) * 128], identity=ident_f[:]),
                                reads=[t_xt[k], t_const], writes=[t_pb[bank]])
                        dst = hT[:, j, g0 * 128:(g0 + G) * 128]
                        src = pb[bank][:, 0:G * 128]
                        if evq % 2 == 0:
                            S.op("act", lambda e, dst=dst, src=src, j=j, which=which: e.activation(
                                out=dst, in_=src, func=AF.Identity, scale=gvec[:, j, which:which + 1],
                                bias=modsb[:, j, which:which + 1]), reads=[t_pb[bank], t_mod], writes=[thT[blk]])
                        else:
                            S.op("dve", lambda e, dst=dst, src=src, j=j, which=which: e.tensor_scalar(
                                out=dst, in0=src, scalar1=gvec[:, j, which:which + 1], scalar2=modsb[:, j, which:which + 1],
                                op0=ALU.mult, op1=ALU.add), reads=[t_pb[bank], t_mod], writes=[thT[blk]])
                        evq += 1
        S.fence()

        def inproj_fm(bank, w_ap_fn, hT, tok0, n, t_w, t_h):
            for j in range(KD):
                S.op("pe", lambda e, j=j: e.matmul(pb[bank][:, 0:n], lhsT=w_ap_fn(j), rhs=hT[:, j, tok0:tok0 + n],
                                                   start=(j == 0), stop=(j == KD - 1)),
                     reads=[t_w] + t_h, writes=[t_pb[bank]])

        def vproj(w_ap_fn, hT, tok0, ntile, vdst, vt0, t_w, t_h, t_v, on_dve=False, alt_bank=False):
            vb, t_vb = (pbt[:, :].bitcast(F32), t_pbt) if alt_bank else (pb[4][:, :], t_pb[4])
            for i in range(ntile):
                for j in range(KD):
                    S.op("pe", lambda e, i=i, j=j: e.matmul(
                        vb[:, i * 128:(i + 1) * 128], lhsT=hT[:, j, tok0 + i * 128:tok0 + (i + 1) * 128], rhs=w_ap_fn(j),
                        start=(j == 0), stop=(j == KD - 1)), reads=[t_w] + t_h, writes=[t_vb])
            if on_dve:
                S.op("dve", lambda e: e.tensor_copy(out=vdst[:, vt0:vt0 + ntile, :],
                                                    in_=vb[:, 0:ntile * 128].rearrange("p (a b) -> p a b", b=128)),
                     reads=[t_vb], writes=[t_v])
            else:
                S.op("act", lambda e: e.activation(out=vdst[:, vt0:vt0 + ntile, :],
                                                   in_=vb[:, 0:ntile * 128].rearrange("p (a b) -> p a b", b=128), func=AF.Copy),
                     reads=[t_vb], writes=[t_v])

        t_pbt = S.tok("pbt")

        with ExitStack() as ph:
            wp = [sb(ph, f"wp{i}", [128, 3, KD, 128], BF16) for i in range(2)]
            t_wp = S.toks("wp", 2)
            scr = Pool_(S, nc, ph, "scr1_", 12, [128, 516], F32)
            ones512 = sb(ph, "ones512", [128, 512]); t_ones = S.tok("ones512")
            S.op("pool", lambda e: e.memset(ones512[:], 1.0), writes=[t_ones])
            v_pre = [sb(ph, f"v_pre{i}", [128, NCP, 128], BF16) for i in range(2)]
            t_vp = S.toks("vp", 2)
            v_ctx = [sb(ph, f"v_ctx{i}", [128, NCX, 128], BF16) for i in range(2)]
            t_vc = S.toks("vc", 2)
            khat = Pool_(S, nc, ph, "khat_", 4, [128, 512], BF16)
            ktokb = [sb(ph, f"ktokb{i}", [128, 4, 128], BF16) for i in range(4)]
            t_ktokb = S.toks("ktokb", 4)
            tsc = Pool_(S, nc, ph, "tsc_", 8, [128, 2], F32)

            def load_wp(hd):
                S.dma("pool", wp[hd % 2][:].rearrange("p a b c -> p (a b c)"), wpre_d[hd, :, :], writes=[t_wp[hd % 2]],
                      max_dma_last_dim=4096)
            all_groups = []
            for hd in range(NH):
                p2 = hd % 2
                jobs = [(hd, 0, hT_ctx, [t_hTc], 0, T_CTX, "suf", 0, hd * 2 + 0, v_ctx[p2], t_vc[p2], 0, True, True),
                        (hd, 1, hT_ctx, [t_hTc], 0, T_CTX, "exc", 1, hd * 2 + 1, v_ctx[p2], t_vc[p2], 0, True, False)]
                for blk in range(NBP):
                    jobs.append((hd, 0, hT_pre, [t_hTp[blk]], blk * 512, 512, "suf", 0, hd * 2 + 0, v_pre[p2], t_vp[p2], blk * 4, False, True))
                for g0 in range(0, len(jobs), 2):
                    all_groups.append(jobs[g0:g0 + 2])
            loaded = set()

            def ensure_w(hd):
                for h in (hd, hd + 1):
                    if h < NH and h not in loaded:
                        load_wp(h)
                        loaded.add(h)

            def P1(gi, grp):
                zb0 = 0 if gi % 2 == 0 else 2
                st = []
                for bi, (hd, wg, hT, th, tok0, n, form, lbdir, si, vbuf, t_v, vt0, first, do_v) in enumerate(grp):
                    ensure_w(hd)
                    w, tw = wp[hd % 2], t_wp[hd % 2]
                    zb = zb0 + bi
                    inproj_fm(zb, lambda j, wg=wg, w=w: w[:, wg, j, :], hT, tok0, n, tw, th)
                    sg, t_sg = scr.get()
                    S.op("act", lambda e, sg=sg, zb=zb, n=n: e.activation(out=sg[:, 0:n], in_=pb[zb][:, 0:n], func=AF.Sigmoid, scale=-1.0),
                         reads=[t_pb[zb]], writes=[t_sg])
                    if do_v:
                        vproj(lambda j, w=w: w[:, 2, j, :], hT, tok0, n // 128, vbuf, vt0, tw, th, t_v, on_dve=True, alt_bank=(bi == 1))
                    st.append([sg, t_sg])
                return st

            def P2e(gi, grp, st):
                for bi, (hd, wg, hT, th, tok0, n, form, lbdir, si, vbuf, t_v, vt0, first, do_v) in enumerate(grp):
                    sg, t_sg = st[bi][0], st[bi][1]
                    lf, t_lf = scr.get(); bb, t_bb = scr.get()
                    kh, t_kh = khat.get()
                    e1t, t_e1 = tsc.get()
                    st[bi] += [lf, t_lf, bb, t_bb, kh, t_kh, e1t, t_e1]
                    if form == "exc":
                        S.op("pool", lambda e, lf=lf: e.memset(lf[:, 0:1], 0.0), writes=[t_lf])
                    S.op("act", lambda e, lf=lf, sg=sg, n=n, lbdir=lbdir, hd=hd: e.activation(
                        out=lf[:, 1:n + 1], in_=sg[:, 0:n], func=AF.Ln, scale=noml[:, lbdir, hd:hd + 1], bias=1.0),
                        reads=[t_sg, t_lb], writes=[t_lf])
                for bi, (hd, wg, hT, th, tok0, n, form, lbdir, si, vbuf, t_v, vt0, first, do_v) in enumerate(grp):
                    sg, t_sg, lf, t_lf, bb, t_bb, kh, t_kh, e1t, t_e1 = st[bi]
                    if form == "suf":
                        S.op("dve", lambda e, lf=lf, bb=bb, n=n: e.tensor_tensor_scan(
                            out=bb[:, 0:n], data0=ones512[:, 0:n], data1=lf[:, 1:n + 1], initial=0.0, op0=ALU.mult, op1=ALU.add),
                            reads=[t_lf, t_ones], writes=[t_bb])
                        S.op("dve", lambda e, bb=bb, e1t=e1t, n=n: e.tensor_copy(out=e1t[:, 1:2], in_=bb[:, n - 1:n]),
                             reads=[t_bb], writes=[t_e1])
                    else:
                        S.op("dve", lambda e, lf=lf, bb=bb, n=n: e.tensor_tensor_scan(
                            out=bb[:, 0:n], data0=lf[:, 0:n], data1=ones512[:, 0:n], initial=0.0, op0=ALU.add, op1=ALU.mult),
                            reads=[t_lf, t_ones], writes=[t_bb])
                for bi, (hd, wg, hT, th, tok0, n, form, lbdir, si, vbuf, t_v, vt0, first, do_v) in enumerate(grp):
                    sg, t_sg, lf, t_lf, bb, t_bb, kh, t_kh, e1t, t_e1 = st[bi]
                    if form == "suf":
                        if not first:
                            S.op("act", lambda e, e1t=e1t: e.activation(out=e1t[:, 0:1], in_=e1t[:, 1:2], func=AF.Exp),
                                 reads=[t_e1], writes=[t_e1])
                        S.op("act", lambda e, bb=bb, e1t=e1t, n=n: e.activation(out=bb[:, 0:n], in_=bb[:, 0:n], func=AF.Exp,
                                                                               scale=-1.0, bias=e1t[:, 1:2]),
                             reads=[t_bb, t_e1], writes=[t_bb])
                    else:
                        S.op("act", lambda e, bb=bb, n=n: e.activation(out=bb[:, 0:n], in_=bb[:, 0:n], func=AF.Exp),
                             reads=[t_bb], writes=[t_bb])
                for bi, (hd, wg, hT, th, tok0, n, form, lbdir, si, vbuf, t_v, vt0, first, do_v) in enumerate(grp):
                    sg, t_sg, lf, t_lf, bb, t_bb, kh, t_kh, e1t, t_e1 = st[bi]
                    S.op("dve", lambda e, kh=kh, sg=sg, bb=bb, n=n, lbdir=lbdir, hd=hd: e.scalar_tensor_tensor(
                        out=kh[:, 0:n], in0=sg[:, 0:n], scalar=oml[:, lbdir, hd:hd + 1], in1=bb[:, 0:n], op0=ALU.mult, op1=ALU.mult),
                        reads=[t_sg, t_bb, t_lb], writes=[t_kh])

            def P2pe(gi, grp, st):
                zb0 = 0 if gi % 2 == 0 else 2
                kslot = (gi % 2) * 2
                pbanks = [(pb[5][:, 0:128], t_pb[5]), (pb[6][:, 0:128], t_pb[6])]
                for bi, (hd, wg, hT, th, tok0, n, form, lbdir, si, vbuf, t_v, vt0, first, do_v) in enumerate(grp):
                    sg, t_sg, lf, t_lf, bb, t_bb, kh, t_kh, e1t, t_e1 = st[bi]
                    zb = zb0 + bi
                    tb = pb[zb][:, :].bitcast(BF16)
                    for i in range(n // 128):
                        S.op("pe", lambda e, i=i, kh=kh, tb=tb: e.transpose(out=tb[:, i * 128:(i + 1) * 128], in_=kh[:, i * 128:(i + 1) * 128],
                                                                          identity=ident_b[:]), reads=[t_kh, t_const], writes=[t_pb[zb]])
                for bi, (hd, wg, hT, th, tok0, n, form, lbdir, si, vbuf, t_v, vt0, first, do_v) in enumerate(grp):
                    nch = n // 128
                    zb = zb0 + bi
                    tb = pb[zb][:, :].bitcast(BF16)
                    kb, t_kb = ktokb[kslot + bi], t_ktokb[kslot + bi]
                    S.op("dve", lambda e, kb=kb, nch=nch, n=n, tb=tb: e.tensor_copy(
                        out=kb[:, 0:nch, :], in_=tb[:, 0:n].rearrange("p (a b) -> p a b", b=128)),
                        reads=[t_pb[zb]], writes=[t_kb])
                for bi, (hd, wg, hT, th, tok0, n, form, lbdir, si, vbuf, t_v, vt0, first, do_v) in enumerate(grp):
                    nch = n // 128
                    kb, t_kb = ktokb[kslot + bi], t_ktokb[kslot + bi]
                    pap, t_p = pbanks[bi]
                    for i in range(nch):
                        S.op("pe", lambda e, i=i, kb=kb, vbuf=vbuf, vt0=vt0, nch=nch, pap=pap: e.matmul(
                            pap, lhsT=kb[:, i, :], rhs=vbuf[:, vt0 + i, :], start=(i == 0), stop=(i == nch - 1)),
                            reads=[t_kb, t_v], writes=[t_p])
                for bi, (hd, wg, hT, th, tok0, n, form, lbdir, si, vbuf, t_v, vt0, first, do_v) in enumerate(grp):
                    sg, t_sg, lf, t_lf, bb, t_bb, kh, t_kh, e1t, t_e1 = st[bi]
                    pap, t_p = pbanks[bi]
                    if first:
                        S.op("dve", lambda e, si=si, pap=pap: e.tensor_copy(out=Sst[:, si, :], in_=pap),
                             reads=[t_p], writes=[t_S[si]])
                    else:
                        S.op("dve", lambda e, si=si, e1t=e1t, pap=pap: e.scalar_tensor_tensor(
                            out=Sst[:, si, :], in0=Sst[:, si, :], scalar=e1t[:, 0:1], in1=pap, op0=ALU.mult, op1=ALU.add),
                            reads=[t_p, t_e1, t_S[si]], writes=[t_S[si]])

            prev = None
            for gi, grp in enumerate(all_groups):
                st = P1(gi, grp)
                if prev is not None:
                    P2pe(*prev)
                P2e(gi, grp, st)
                prev = (gi, grp, st)
            P2pe(*prev)
        st1.close()
        S.fence()

        ymixT = sb(es, "ymixT", [128, NM, T_OWN], BF16)
        t_ym = [S.toks(f"ym{m}_", NBO) for m in range(NM)]

        with ExitStack() as ph:
            wh = [sb(ph, f"wh{i}", [128, 5, KD, 128], BF16) for i in range(2)]
            t_wh = S.toks("wh", 2)
            scr = Pool_(S, nc, ph, "scr2_", 12, [128, 516], F32)
            for tl, tt in zip(scr.tiles, scr.toks):
                S.op("pool", lambda e, tl=tl: e.memset(tl[:], 0.0), writes=[tt])
            v_hd = [sb(ph, f"v_hd{i}", [128, NCO, 128], BF16) for i in range(1)]
            t_vh = S.toks("vh", 1)
            Q = sb(ph, "qk", [128, 6, T_OWN], BF16)
            TQ = [S.toks(f"qk{a}_", NBO) for a in range(6)]
            sqb = [sb(ph, f"sqb{i}", [128, 512], BF16) for i in range(2)]; t_sqb = S.toks("sqb", 2)
            ones_b = sb(ph, "ones_b", [128, 128], BF16); t_onesb = S.tok("ones_b")
            S.op("pool", lambda e: e.memset(ones_b[:], 1.0), writes=[t_onesb])
            silu_g = sb(ph, "silu_g", [128, T_OWN], BF16); t_sil = S.toks("sil", NBO)
            escA = sb(ph, "escA", [128, 3, NCO]); escB = sb(ph, "escB", [128, 3, NCO])
            t_escA, t_escB = S.tok("escA"), S.tok("escB")
            SpB = sb(ph, "SpB", [128, NCO, 128], BF16); t_SpB = S.toks("SpB", NCO)
            SpA = [sb(ph, f"SpA{i}", [128, 128], BF16) for i in range(2)]; t_SpA = S.toks("SpA", 2)
            atm = [sb(ph, f"atm{i}", [128, 2, 128], BF16) for i in range(2)]; t_atm = S.toks("atm", 2)
            ktokb = [sb(ph, f"ktokc{i}", [128, 4, 128], BF16) for i in range(2)]
            t_ktokb = S.toks("ktokc", 2)
            Spp = sb(ph, "Spp", [128, 2, 3, 128]); t_Spp = [S.toks("SppA", 3), S.toks("SppB", 3)]
            grpctr = [0]

            class Ring:
                def __init__(self, items):
                    self.items, self.i = items, 0

                def get(self):
                    it = self.items[self.i % len(self.items)]
                    self.i += 1
                    return it
            xt_tiles = []
            for p in range(min(8, NCC)):
                pl = ymixT[:, NH + p, :].bitcast(F32)
                for hh in range(T_OWN // 1024):
                    xt_tiles.append((pl[:, hh * 512:(hh + 1) * 512], S.tok(f"xscr{p}_{hh}")))
            assert len(xt_tiles) >= 4 + 3 * NBO, "need the 8 conv planes of ymixT as scratch"
            xr_sgA, xr_sgB = Ring(xt_tiles[0:2]), Ring(xt_tiles[2:4])
            xr_qf, xr_bbA, xr_bbB = [Ring(xt_tiles[4 + i * NBO:4 + (i + 1) * NBO]) for i in range(3)]
            xscr_toks = [t for _, t in xt_tiles]

            def load_wh(hd):
                S.dma("pool", wh[hd % 2][:].rearrange("p a b c -> p (a b c)"), whd_d[hd, :, :], writes=[t_wh[hd % 2]],
                      max_dma_last_dim=4096)

            def bulk_P(qi, cg, bank, tb, t_tb, vh, t_v):
                gn = grpctr[0]
                grpctr[0] += 1
                kb, t_kb = ktokb[gn % 2], t_ktokb[gn % 2]
                for i in range(4):
                    c = cg * 4 + i
                    S.op("pe", lambda e, i=i, c=c: e.transpose(out=tb[:, i * 128:(i + 1) * 128], in_=Q[:, qi, c * 128:(c + 1) * 128],
                                                              identity=ident_b[:]), reads=[TQ[qi][cg], t_const], writes=[t_tb])
                S.op("act", lambda e: e.activation(out=kb[:], in_=tb[:, 0:512].rearrange("p (a b) -> p a b", b=128), func=AF.Copy),
                     reads=[t_tb], writes=[t_kb])
                for i in range(4):
                    c = cg * 4 + i
                    S.op("pe", lambda e, i=i, c=c: e.matmul(pb[bank][:, i * 128:(i + 1) * 128], lhsT=kb[:, i, :], rhs=vh[:, c, :],
                                                            start=True, stop=True), reads=[t_kb, t_v], writes=[t_pb[bank]])

            load_wh(0)
            for hd in range(NH):
                if hd + 1 < NH:
                    load_wh(hd + 1)
                w, tw = wh[hd % 2], t_wh[hd % 2]
                p2 = hd % 2
                vh, t_v = v_hd[0], t_vh[0]
                siA, siB = hd * 2, hd * 2 + 1
                NG = NCO // 4
                Bst = {"cur": Sst[:, siB, :], "t": t_S[siB], "k": 0}

                def B_bulk(cg, k=0, vh=vh, t_v=t_v):
                    tb, t_tb = (pbt, t_pbt) if k % 2 == 0 else (pb[5][:, :].bitcast(BF16), t_pb[5])
                    bulk_P(5, cg, cg % 4, tb, t_tb, vh, t_v)

                def B_stage(cg, Bst=Bst, vh=vh, t_v=t_v, siB=siB, with_bulk=True):
                    pbk = cg % 4
                    if with_bulk:
                        bulk_P(5, cg, pbk, pbt, t_pbt, vh, t_v)
                    for c in reversed(range(cg * 4, cg * 4 + 4)):
                        curB, t_curB = Bst["cur"], Bst["t"]
                        S.op("act", lambda e, c=c, curB=curB: e.activation(out=SpB[:, c, :], in_=curB, func=AF.Copy, scale=escB[:, 2, c:c + 1]),
                             reads=[t_curB, t_escB], writes=[t_SpB[c]])
                        if c > 0:
                            k = Bst["k"]
                            Bst["k"] += 1
                            nxt, t_nxt = Spp[:, 1, k % 3, :], t_Spp[1][k % 3]
                            S.op("dve", lambda e, c=c, curB=curB, nxt=nxt: e.scalar_tensor_tensor(
                                out=nxt, in0=curB, scalar=escB[:, 0, c:c + 1], in1=pb[pbk][:, (c % 4) * 128:(c % 4 + 1) * 128],
                                op0=ALU.mult, op1=ALU.add), reads=[t_pb[pbk], t_escB, t_curB], writes=[t_nxt])
                            Bst["cur"], Bst["t"] = nxt, t_nxt
                blkst = {}

                def A_pe_e1(blk, pend):
                    tk0 = blk * 512
                    th = [t_hTo[blk]]
                    d = {}
                    for g in (2, 1, 3, 0):
                        inproj_fm(g, lambda j, g=g: w[:, g, j, :], hT_own, tk0, 512, tw, th)
                    d["sgA"], d["t_sgA"] = xr_sgA.get(); d["sgB"], d["t_sgB"] = xr_sgB.get()
                    d["qf"], d["t_qf"] = xr_qf.get()
                    sgg, t_sgg = scr.get()
                    sgA, sgB, qf = d["sgA"], d["sgB"], d["qf"]
                    S.op("act", lambda e: e.activation(out=sgB[:, 0:512], in_=pb[2][:, :], func=AF.Sigmoid, scale=-1.0),
                         reads=[t_pb[2]], writes=[d["t_sgB"]])
                    S.op("act", lambda e: e.activation(out=sgA[:, 0:512], in_=pb[1][:, :], func=AF.Sigmoid, scale=-1.0),
                         reads=[t_pb[1]], writes=[d["t_sgA"]])
                    S.op("act", lambda e: e.activation(out=sgg[:, 0:512], in_=pb[3][:, :], func=AF.Sigmoid),
                         reads=[t_pb[3]], writes=[t_sgg])
                    S.op("act", lambda e: e.activation(out=qf[:, 0:512], in_=pb[0][:, :], func=AF.Copy),
                         reads=[t_pb[0]], writes=[d["t_qf"]])
                    S.op("dve", lambda e: e.tensor_tensor(out=silu_g[:, tk0:tk0 + 512], in0=pb[3][:, :], in1=sgg[:, 0:512], op=ALU.mult),
                         reads=[t_pb[3], t_sgg], writes=[t_sil[blk]])
                    vproj(lambda j: w[:, 4, j, :], hT_own, tk0, 4, vh, blk * 4, tw, th, t_v)
                    if pend is not None:
                        B_bulk(pend)
                    blkst[blk] = d

                def A_e2(blk):
                    d = blkst[blk]
                    sgA, sgB = d["sgA"], d["sgB"]
                    lfA, t_lfA = scr.get(); lfB, t_lfB = scr.get()
                    d["bbA"], d["t_bbA"] = xr_bbA.get(); d["bbB"], d["t_bbB"] = xr_bbB.get()
                    bbA, bbB, t_bbA, t_bbB = d["bbA"], d["bbB"], d["t_bbA"], d["t_bbB"]
                    S.op("act", lambda e: e.activation(out=lfB[:, 1:513], in_=sgB[:, 0:512], func=AF.Ln,
                                                       scale=noml[:, 1, hd:hd + 1], bias=1.0), reads=[d["t_sgB"], t_lb], writes=[t_lfB])
                    S.op("act", lambda e: e.activation(out=lfA[:, 1:513], in_=sgA[:, 0:512], func=AF.Ln,
                                                       scale=noml[:, 0, hd:hd + 1], bias=1.0), reads=[d["t_sgA"], t_lb], writes=[t_lfA])
                    S.op("dve", lambda e: e.tensor_tensor_scan(out=bbB[:, 0:512], data0=lfB[:, 0:512], data1=scanmask[:, :],
                                                               initial=0.0, op0=ALU.add, op1=ALU.mult),
                         reads=[t_lfB, t_const], writes=[t_bbB])
                    S.op("dve", lambda e: e.tensor_tensor_scan(out=bbA[:, 0:512], data0=scanmask[:, :], data1=lfA[:, 1:513],
                                                               initial=0.0, op0=ALU.mult, op1=ALU.add),
                         reads=[t_lfA, t_const], writes=[t_bbA])
                    scA, t_scA = scs.get(); scB, t_scB = scs.get()
                    d["scA"], d["t_scA"], d["scB"], d["t_scB"] = scA, t_scA, scB, t_scB
                    a3 = bbA[:, 0:512].rearrange("p (c t) -> p c t", t=128)
                    b3 = bbB[:, 0:512].rearrange("p (c t) -> p c t", t=128)
                    lfB3 = lfB[:, 1:513].rearrange("p (c t) -> p c t", t=128)
                    sA3 = scA[:, 0:12].rearrange("p (a b) -> p a b", b=4)
                    sB3 = scB[:, 0:12].rearrange("p (a b) -> p a b", b=4)
                    S.op("dve", lambda e: e.tensor_tensor(out=sB3[:, 0, :], in0=b3[:, :, 127], in1=lfB3[:, :, 127], op=ALU.add),
                         reads=[t_bbB, t_lfB], writes=[t_scB])
                    S.op("dve", lambda e: e.tensor_copy(out=sB3[:, 1, :], in_=b3[:, :, 64]), reads=[t_bbB], writes=[t_scB])
                    S.op("dve", lambda e: e.tensor_tensor(out=sB3[:, 2, :], in0=sB3[:, 0, :], in1=b3[:, :, 64], op=ALU.subtract),
                         reads=[t_bbB, t_scB], writes=[t_scB])
                    S.op("dve", lambda e: e.tensor_copy(out=sA3[:, 0, :], in_=a3[:, :, 127]), reads=[t_bbA], writes=[t_scA])
                    S.op("dve", lambda e: e.tensor_tensor(out=sA3[:, 1, :], in0=a3[:, :, 127], in1=a3[:, :, 63], op=ALU.subtract),
                         reads=[t_bbA], writes=[t_scA])
                    S.op("dve", lambda e: e.tensor_copy(out=sA3[:, 2, :], in_=a3[:, :, 63]), reads=[t_bbA], writes=[t_scA])
                    S.op("pool", lambda e: e.tensor_tensor(
                        out=b3, in0=b3, in1=scB[:, 4:8].rearrange("p (c o) -> p c o", o=1).to_broadcast([128, 4, 128]), op=ALU.subtract),
                        reads=[t_bbB, t_scB], writes=[t_bbB])
                    S.op("pool", lambda e: e.tensor_tensor(
                        out=a3, in0=a3, in1=scA[:, 8:12].rearrange("p (c o) -> p c o", o=1).to_broadcast([128, 4, 128]), op=ALU.subtract),
                        reads=[t_bbA, t_scA], writes=[t_bbA])

                def A_e3(blk):
                    d = blkst[blk]
                    tk0 = blk * 512
                    c0 = blk * 4
                    sgA, sgB, qf, bbA, bbB = d["sgA"], d["sgB"], d["qf"], d["bbA"], d["bbB"]
                    t_bbA, t_bbB = d["t_bbA"], d["t_bbB"]
                    sA3 = d["scA"][:, 0:12].rearrange("p (a b) -> p a b", b=4)
                    sB3 = d["scB"][:, 0:12].rearrange("p (a b) -> p a b", b=4)
                    S.op("act", lambda e: e.activation(out=escB[:, :, c0:c0 + 4], in_=sB3, func=AF.Exp), reads=[d["t_scB"]], writes=[t_escB])
                    S.op("act", lambda e: e.activation(out=escA[:, :, c0:c0 + 4], in_=sA3, func=AF.Exp), reads=[d["t_scA"]], writes=[t_escA])
                    ekA, t_ekA = scr.get(); ekB, t_ekB = scr.get()
                    S.op("act", lambda e: e.activation(out=ekB[:, 0:512], in_=bbB[:, 0:512], func=AF.Exp), reads=[t_bbB], writes=[t_ekB])
                    S.op("act", lambda e: e.activation(out=ekA[:, 0:512], in_=bbA[:, 0:512], func=AF.Exp, scale=-1.0), reads=[t_bbA], writes=[t_ekA])
                    S.op("dve", lambda e: e.scalar_tensor_tensor(
                        out=Q[:, 3, tk0:tk0 + 512], in0=sgB[:, 0:512], scalar=oml[:, 1, hd:hd + 1], in1=ekB[:, 0:512], op0=ALU.mult, op1=ALU.mult),
                        reads=[d["t_sgB"], t_ekB, t_lb], writes=[TQ[3][blk]])
                    S.op("dve", lambda e: e.scalar_tensor_tensor(
                        out=Q[:, 2, tk0:tk0 + 512], in0=sgA[:, 0:512], scalar=oml[:, 0, hd:hd + 1], in1=ekA[:, 0:512], op0=ALU.mult, op1=ALU.mult),
                        reads=[d["t_sgA"], t_ekA, t_lb], writes=[TQ[2][blk]])
                    for (src, dst, esc, t_esc) in ((3, 5, escB, t_escB), (2, 4, escA, t_escA)):
                        S.op("pool", lambda e, src=src, dst=dst, esc=esc: e.tensor_tensor(
                            out=Q[:, dst, tk0:tk0 + 512].rearrange("p (c t) -> p c t", t=128),
                            in0=Q[:, src, tk0:tk0 + 512].rearrange("p (c t) -> p c t", t=128),
                            in1=esc[:, 1, c0:c0 + 4].rearrange("p (c o) -> p c o", o=1).to_broadcast([128, 4, 128]), op=ALU.mult),
                            reads=[TQ[src][blk], t_esc], writes=[TQ[dst][blk]])

                def A_q(blk):
                    d = blkst[blk]
                    tk0 = blk * 512
                    qf, bbA, bbB = d["qf"], d["bbA"], d["bbB"]
                    eqA, t_eqA = scr.get(); eqB, t_eqB = scr.get()
                    S.op("act", lambda e: e.activation(out=eqA[:, 0:512], in_=bbA[:, 0:512], func=AF.Exp), reads=[d["t_bbA"]], writes=[t_eqA])
                    S.op("act", lambda e: e.activation(out=eqB[:, 0:512], in_=bbB[:, 0:512], func=AF.Exp, scale=-1.0), reads=[d["t_bbB"]], writes=[t_eqB])
                    S.op("pool", lambda e: e.tensor_tensor(out=Q[:, 0, tk0:tk0 + 512], in0=qf[:, 0:512], in1=eqA[:, 0:512], op=ALU.mult),
                         reads=[d["t_qf"], t_eqA], writes=[TQ[0][blk]])
                    S.op("pool", lambda e: e.tensor_tensor(out=Q[:, 1, tk0:tk0 + 512], in0=qf[:, 0:512], in1=eqB[:, 0:512], op=ALU.mult),
                         reads=[d["t_qf"], t_eqB], writes=[TQ[1][blk]])

                bq = []
                prev_blk = None
                for blk in reversed(range(NBO)):
                    A_pe_e1(blk, None)
                    if prev_blk is not None:
                        A_e3(prev_blk)
                        bq.append(prev_blk)
                    A_e2(blk)
                    prev_blk = blk
                for k, cgp in enumerate(bq):
                    B_bulk(cgp, k)
                A_e3(prev_blk)
                A_q(0)
                for cgp in bq:
                    B_stage(cgp, with_bulk=False)
                B_stage(prev_blk)
                for cg in range(min(2, NG)):
                    tb, t_tb = (pbt, t_pbt) if cg % 2 == 0 else (pb[4][:, :].bitcast(BF16), t_pb[4])
                    bulk_P(4, cg, cg % 2, tb, t_tb, vh, t_v)
                later = []
                curA = [Sst[:, siA, :], t_S[siA]]

                def emit_AT(c):
                    blk = c // 4
                    cs = slice(c * 128, (c + 1) * 128)
                    ab = 5 if c % 2 == 0 else 2
                    am, t_am = atm[c % 2], t_atm[c % 2]
                    S.op("pe", lambda e: e.matmul(pb[ab][:, 0:128], lhsT=Q[:, 2, cs], rhs=Q[:, 0, cs], start=True, stop=True),
                         reads=[TQ[2][blk], TQ[0][blk]], writes=[t_pb[ab]])
                    S.op("pe", lambda e: e.matmul(pb[ab][:, 128:256], lhsT=Q[:, 3, cs], rhs=Q[:, 1, cs], start=True, stop=True),
                         reads=[TQ[3][blk], TQ[1][blk]], writes=[t_pb[ab]])
                    S.op("dve", lambda e: e.tensor_tensor(out=am[:].rearrange("p a b -> p (a b)"), in0=pb[ab][:, 0:256],
                                                          in1=mask2[:].rearrange("p a b -> p (a b)"), op=ALU.mult),
                         reads=[t_pb[ab], t_const], writes=[t_am])

                def emit_o(c):
                    blk, cq = c // 4, c % 4
                    cs = slice(c * 128, (c + 1) * 128)
                    ob = 6 if blk % 2 == 0 else 3
                    am, t_am = atm[c % 2], t_atm[c % 2]
                    spa, t_spa = SpA[c % 2], t_SpA[c % 2]
                    cA, t_cA = curA
                    S.op("act", lambda e: e.activation(out=spa[:], in_=cA, func=AF.Copy, scale=escA[:, 2, c:c + 1]),
                         reads=[t_cA, t_escA], writes=[t_spa])
                    if c < NCO - 1:
                        nxt, t_nxt = Spp[:, 0, c % 3, :], t_Spp[0][c % 3]
                        pbk = (c // 4) % 2
                        S.op("dve", lambda e: e.scalar_tensor_tensor(out=nxt, in0=cA, scalar=escA[:, 0, c:c + 1],
                                                                     in1=pb[pbk][:, (c % 4) * 128:(c % 4 + 1) * 128], op0=ALU.mult, op1=ALU.add),
                             reads=[t_pb[pbk], t_escA, t_cA], writes=[t_nxt])
                        curA[0], curA[1] = nxt, t_nxt
                    oslc = pb[ob][:, cq * 128:(cq + 1) * 128]
                    S.op("pe", lambda e: e.matmul(oslc, lhsT=vh[:, c, :], rhs=am[:, 0, :], start=True, stop=False),
                         reads=[t_v, t_am], writes=[t_pb[ob]])
                    S.op("pe", lambda e: e.matmul(oslc, lhsT=vh[:, c, :], rhs=am[:, 1, :], start=False, stop=False),
                         reads=[t_v, t_am], writes=[t_pb[ob]])
                    S.op("pe", lambda e: e.matmul(oslc, lhsT=spa[:], rhs=Q[:, 0, cs], start=False, stop=False),
                         reads=[t_spa, TQ[0][blk]], writes=[t_pb[ob]])
                    S.op("pe", lambda e: e.matmul(oslc, lhsT=SpB[:, c, :], rhs=Q[:, 1, cs], start=False, stop=True),
                         reads=[t_SpB[c], TQ[1][blk]], writes=[t_pb[ob]])
                    if cq == 3:
                        tk0 = blk * 512
                        sq, t_sq = sqb[blk % 2], t_sqb[blk % 2]
                        rs, t_rs = scr.get(); t1, t_t1 = scr.get()
                        S.op("act", lambda e: e.activation(out=sq[:, 0:512], in_=pb[ob][:, :], func=AF.Square),
                             reads=[t_pb[ob]], writes=[t_sq])

                        def p2a():
                            S.op("pe", lambda e: e.matmul(pb[4][:, :], lhsT=ones_b[:], rhs=sq[:, 0:512], start=True, stop=True),
                                 reads=[t_sq, t_onesb], writes=[t_pb[4]])

                        def p2b():
                            S.op("act", lambda e: e.activation(out=rs[:, 0:512], in_=pb[4][:, :], func=AF.Ln, scale=1.0 / 128, bias=EPS),
                                 reads=[t_pb[4]], writes=[t_rs])
                            S.op("act", lambda e: e.activation(out=rs[:, 0:512], in_=rs[:, 0:512], func=AF.Exp, scale=-0.5),
                                 reads=[t_rs], writes=[t_rs])

                        def p2c():
                            S.op("dve", lambda e: e.scalar_tensor_tensor(out=t1[:, 0:512], in0=pb[ob][:, :], scalar=onw[:, 0:1],
                                                                         in1=rs[:, 0:512], op0=ALU.mult, op1=ALU.mult),
                                 reads=[t_pb[ob], t_rs, t_misc], writes=[t_t1])
                            S.op("pool", lambda e: e.tensor_tensor(out=ymixT[:, hd, tk0:tk0 + 512], in0=t1[:, 0:512],
                                                                   in1=silu_g[:, tk0:tk0 + 512], op=ALU.mult),
                                 reads=[t_t1, t_sil[blk]], writes=[t_ym[hd][blk]])
                        later.append([1, p2a]); later.append([2, p2b]); later.append([3, p2c])

                emit_AT(0)
                for c in range(NCO):
                    if c % 4 == 0 and c // 4 + 1 < NBO:
                        A_q(c // 4 + 1)
                    if c + 1 < NCO:
                        emit_AT(c + 1)
                    emit_o(c)
                    if c % 4 == 3 and c // 4 + 2 < NG:
                        cg2 = c // 4 + 2
                        bulk_P(4, cg2, cg2 % 2, pbt, t_pbt, vh, t_v)
                    for it in later:
                        it[0] -= 1
                    for it in [it for it in later if it[0] < 0]:
                        it[1]()
                        later.remove(it)
                for it in later:
                    it[1]()
        S.fence()
        for cc in range(NCC):
            for t in t_ym[NH + cc]:
                t.r = list(S.cur_fence)

        wout = sb(es, "wout", [128, NM, D], BF16); t_wout = S.tok("wout")
        with ExitStack() as ph:
            wc = [sb(ph, f"wc{i}", [128, 4, KD, 128], BF16) for i in range(2)]
            t_wc = S.toks("wc", 2)
            scr = Pool_(S, nc, ph, "scr3_", 12, [128, 516], F32)

            def load_wc(cc):
                S.dma("pool", wc[cc % 2][:].rearrange("p a b c -> p (a b c)"), wcv_d[cc, :, :], writes=[t_wc[cc % 2]],
                      max_dma_last_dim=4096)
            load_wc(0)
            for cc in range(NCC):
                if cc + 1 < NCC:
                    load_wc(cc + 1)
                if cc == 0:
                    S.dma("pool", wout[:].rearrange("p a b -> p (a b)"), wout_d[:, :], writes=[t_wout], max_dma_last_dim=4096)
                w, tw = wc[cc % 2], t_wc[cc % 2]
                for blk in range(NBO):
                    tk0 = blk * 512
                    th = [t_hTo[blk]]
                    if (cc * NBO + blk) % 2 == 0:
                        cb = [(pb[i][:, :], t_pb[i]) for i in range(4)]
                    else:
                        cb = [(pb[4][:, :], t_pb[4]), (pb[5][:, :], t_pb[5]), (pb[6][:, :], t_pb[6]), (pbt[:, :].bitcast(F32), t_pbt)]
                    for g in range(4):
                        for j in range(KD):
                            S.op("pe", lambda e, j=j, g=g: e.matmul(cb[g][0], lhsT=w[:, g, j, :], rhs=hT_own[:, j, tk0:tk0 + 512],
                                                                    start=(j == 0), stop=(j == KD - 1)),
                                 reads=[tw] + th, writes=[cb[g][1]])
                    hs, t_hs = scr.get(); u, t_u = scr.get(); acc, t_acc = scr.get(); sgg, t_sgg = scr.get()
                    t1, t_t1 = scr.get(); t2, t_t2 = scr.get()
                    S.op("act", lambda e, hs=hs: e.activation(out=hs[:, 0:512], in_=cb[0][0], func=AF.Copy), reads=[cb[0][1]], writes=[t_hs])
                    S.op("act", lambda e, sgg=sgg: e.activation(out=sgg[:, 0:512], in_=cb[3][0], func=AF.Sigmoid), reads=[cb[3][1]], writes=[t_sgg])
                    S.op("dve", lambda e, hs=hs, u=u: e.tensor_tensor(out=u[:, 0:512], in0=cb[2][0], in1=hs[:, 0:512], op=ALU.mult),
                         reads=[cb[2][1], t_hs], writes=[t_u])
                    u3 = u[:, 0:512].rearrange("p (r t) -> p r t", t=GW)
                    a3 = acc[:, 0:512].rearrange("p (r t) -> p r t", t=GW)
                    S.op("pool", lambda e, u=u, acc=acc: e.tensor_scalar(out=acc[:, 0:512], in0=u[:, 0:512], scalar1=convw[:, 1, cc:cc + 1],
                                                                        scalar2=0.0, op0=ALU.mult, op1=ALU.add),
                         reads=[t_u, t_misc], writes=[t_acc])
                    S.op("dve", lambda e, u3=u3, a3=a3: e.scalar_tensor_tensor(out=a3[:, :, 1:GW], in0=u3[:, :, 0:GW - 1], scalar=convw[:, 0, cc:cc + 1],
                                                                                in1=a3[:, :, 1:GW], op0=ALU.mult, op1=ALU.add),
                         reads=[t_u, t_misc, t_acc], writes=[t_acc])
                    S.op("dve", lambda e, u3=u3, a3=a3: e.scalar_tensor_tensor(out=a3[:, :, 0:GW - 1], in0=u3[:, :, 1:GW], scalar=convw[:, 2, cc:cc + 1],
                                                                                in1=a3[:, :, 0:GW - 1], op0=ALU.mult, op1=ALU.add),
                         reads=[t_u, t_misc, t_acc], writes=[t_acc])
                    S.op("dve", lambda e, acc=acc, t1=t1: e.tensor_tensor(out=t1[:, 0:512], in0=cb[1][0], in1=acc[:, 0:512], op=ALU.mult),
                         reads=[cb[1][1], t_acc], writes=[t_t1])
                    S.op("dve", lambda e, sgg=sgg, t2=t2: e.tensor_tensor(out=t2[:, 0:512], in0=cb[3][0], in1=sgg[:, 0:512], op=ALU.mult),
                         reads=[cb[3][1], t_sgg], writes=[t_t2])
                    S.op("dve", lambda e, t1=t1, t2=t2, tk0=tk0: e.tensor_tensor(out=ymixT[:, NH + cc, tk0:tk0 + 512], in0=t1[:, 0:512],
                                                                                 in1=t2[:, 0:512], op=ALU.mult),
                         reads=[t_t1, t_t2], writes=[t_ym[NH + cc][blk]])
        S.fence()

        with ExitStack() as ph:
            NXB = 3
            xr = [sb(ph, f"xr{i}", [128, D]) for i in range(NXB)]; t_xr = S.toks("xr", NXB)
            yt = [sb(ph, f"yt{i}", [128, D]) for i in range(NXB)]; t_yt = S.toks("yt", NXB)
            ot = [sb(ph, f"ot{i}", [128, D]) for i in range(NXB)]; t_ot = S.toks("ot", NXB)
            junk = sb(ph, "junk3", [128, D]); t_junk = S.tok("junk3")
            fs = sb(ph, "fs", [128, 3, NCO]); t_fs = S.toks("fs", NCO)
            t_out = S.toks("out", NCO)
            for i in range(NCO):
                k = i % NXB
                blk = i // 4
                S.dma("sp", xr[k][:], x_own[i * 128:(i + 1) * 128, :], writes=[t_xr[k]])
                for hf in range(NHALF):
                    bank = (i * NHALF + hf) % 4
                    for m in range(NM):
                        S.op("pe", lambda e, m=m, hf=hf, bank=bank: e.matmul(
                            pb[bank][:, 0:HALF], lhsT=ymixT[:, m, i * 128:(i + 1) * 128], rhs=wout[:, m, hf * HALF:(hf + 1) * HALF],
                            start=(m == 0), stop=(m == NM - 1)), reads=[t_ym[m][blk], t_wout], writes=[t_pb[bank]])
                    S.op("dve", lambda e, hf=hf, bank=bank, k=k: e.tensor_tensor(
                        out=yt[k][:, hf * HALF:(hf + 1) * HALF], in0=pb[bank][:, 0:HALF], in1=gate_bc[:, hf * HALF:(hf + 1) * HALF], op=ALU.mult),
                        reads=[t_pb[bank], t_gate], writes=[t_yt[k]])
                S.op("pool", lambda e, k=k: e.tensor_tensor(out=yt[k][:], in0=yt[k][:], in1=xr[k][:], op=ALU.add),
                     reads=[t_yt[k], t_xr[k]], writes=[t_yt[k]])
                S.op("act", lambda e, k=k: e.activation(out=junk[:], in_=yt[k][:], func=AF.Square, accum_out=fs[:, 0, i:i + 1]),
                     reads=[t_yt[k]], writes=[t_junk, t_fs[i]])
                S.op("act", lambda e: e.activation(out=fs[:, 1, i:i + 1], in_=fs[:, 0, i:i + 1], func=AF.Ln, scale=1.0 / D, bias=EPS),
                     reads=[t_fs[i]], writes=[t_fs[i]])
                S.op("act", lambda e: e.activation(out=fs[:, 2, i:i + 1], in_=fs[:, 1, i:i + 1], func=AF.Exp, scale=-0.5),
                     reads=[t_fs[i]], writes=[t_fs[i]])
                S.op("dve", lambda e, k=k: e.scalar_tensor_tensor(out=ot[k][:], in0=yt[k][:], scalar=fs[:, 2, i:i + 1], in1=fw_bc[:],
                                                                  op0=ALU.mult, op1=ALU.mult),
                     reads=[t_yt[k], t_fs[i], t_fw], writes=[t_ot[k]])
                S.dma("act", out_d[i * 128:(i + 1) * 128, :], ot[k][:], reads=[t_ot[k]], writes=[t_out[i]])
            S.wait_all("sp", t_out)
        build.stats = (S.n_inst, S.n_wait)
    return nc


def host_prep(cfg, inp, b, s, HGW, CW):
    D, NH, NCC, KD, NM = cfg.D, cfg.NH, cfg.NCC, cfg.KD, cfg.NM
    f = np.float32
    flip = (s == 0)
    xb = inp["x"][b]
    cx = inp["ctx"][b]
    if flip:
        xb = xb[::-1]
        cx = cx[::-1]
    m = {}
    m["x_pre"] = np.ascontiguousarray(xb[:cfg.T_PRE], f)
    m["x_own"] = np.ascontiguousarray(xb[cfg.T_PRE:], f)
    m["x_ctx"] = np.ascontiguousarray(cx, f)
    cv = np.stack([inp["c"][b].reshape(KD, 128).T, inp["c_ctx"].reshape(KD, 128).T], -1)
    m["cvec"] = np.ascontiguousarray(cv.reshape(128, KD * 2), f)
    wa = inp["w_ada"][0]
    wa = wa.reshape(KD, 128, 3, D).transpose(2, 1, 0, 3)
    m["wada"] = np.ascontiguousarray(wa.reshape(3, 128, KD * D), f)
    ba = inp["b_ada"][0].reshape(3, KD, 128)
    m["bada_fm"] = np.ascontiguousarray(ba[0:2].transpose(2, 0, 1).reshape(128, 2 * KD), f)
    m["bada_gate_bc"] = np.ascontiguousarray(np.broadcast_to(inp["b_ada"][0][2 * D:3 * D], (128, D)), f)
    m["normw_fm"] = np.ascontiguousarray(inp["norm_w"][0].reshape(KD, 128).T, f)
    lg = inp["hg_lb_logits"]
    dA, dB = (1, 0) if flip else (0, 1)
    l2 = np.stack([lg[dA], lg[dB]], 0).reshape(2, 2, NH, 128).transpose(3, 0, 1, 2)
    m["lbl"] = np.ascontiguousarray(l2.reshape(128, 4 * NH), f)
    m["onw"] = np.ascontiguousarray(inp["hg_onorm_w"][0].reshape(128, 1), f)
    cw = inp["conv_w"][0]
    if flip:
        cw = cw[::-1]
    m["convw"] = np.ascontiguousarray(cw.reshape(3, NCC, 128).transpose(2, 0, 1).reshape(128, 3 * NCC), f)
    w = inp["w_in"][0]
    o_q, o_i, o_zf, o_zb, o_g = 0, HGW, 2 * HGW, 3 * HGW, 4 * HGW
    o_zA, o_zB = (o_zb, o_zf) if flip else (o_zf, o_zb)
    oc = 5 * HGW

    def grp(off):
        return w[:, off:off + 128].reshape(KD, 128, 128).transpose(1, 0, 2)
    wpre = np.empty((NH, 128, 3, KD, 128), f)
    whd = np.empty((NH, 128, 5, KD, 128), f)
    for hd in range(NH):
        h0 = hd * 128
        wpre[hd, :, 0], wpre[hd, :, 1], wpre[hd, :, 2] = grp(o_zA + h0), grp(o_zB + h0), grp(o_i + h0)
        for gi, off in enumerate([o_q, o_zA, o_zB, o_g, o_i]):
            whd[hd, :, gi] = grp(off + h0)
    wcv = np.empty((NCC, 128, 4, KD, 128), f)
    for cc in range(NCC):
        for gi in range(4):
            wcv[cc, :, gi] = grp(oc + gi * CW + cc * 128)
    m["w_pre"] = wpre.reshape(NH, 128, 3 * KD * 128)
    m["w_hd"] = whd.reshape(NH, 128, 5 * KD * 128)
    m["w_cv"] = wcv.reshape(NCC, 128, 4 * KD * 128)
    wo = inp["w_out"][0].reshape(NM, 128, D).transpose(1, 0, 2)
    m["w_out"] = np.ascontiguousarray(wo.reshape(128, NM * D), f)
    m["fw_bc"] = np.ascontiguousarray(np.broadcast_to(inp["final_norm_w"], (128, D)), f)
    return m


def assemble(cfg, results, B, SEQ):
    out = np.empty((B, SEQ, cfg.D), np.float32)
    for b in range(B):
        for s in range(2):
            r = results[b * 2 + s]["out"]
            if s == 0:
                out[b, :cfg.T_OWN] = r[::-1]
            else:
                out[b, cfg.T_PRE:] = r
    return out


_NC_CACHE = {}


def kernel(**inputs):
    cfg = FULL
    inp = {k: np.asarray(v) for k, v in inputs.items()}
    B, SEQ, _ = inp["x"].shape
    assert B * 2 == 8 and SEQ == cfg.T_OWN + cfg.T_PRE
    if "nc" not in _NC_CACHE:
        _NC_CACHE["nc"] = build(cfg)
    nc = _NC_CACHE["nc"]
    in_maps = [host_prep(cfg, inp, b, s, 1024, 1024) for b in range(B) for s in range(2)]
    res = run_bass_kernel_spmd(nc, in_maps, core_ids=list(range(8)))
    return assemble(cfg, res.results, B, SEQ)
```
